# Optimizing a Trainium2 kernel written in Bass

```python
import math
import jax, jax.numpy as jnp
from jax import lax
import numpy as np

D_MODEL = 2048
BATCH = 4
SEQ = 2048
DEPTH = 1
DEC_BATCH = 128
DEC_SEQ = 4
PAST_LEN = 2048
PAGE_SIZE = 128

D_MIX = D_MODEL
D_ATT = D_MIX // 2
D_SSD = D_MIX - D_ATT
ATT_DK = 64
ATT_DV = 2 * ATT_DK
N_ATT_HEADS = D_ATT // ATT_DV
ROT_DIM = ATT_DK // 4
ROPE_THETA = 500000.0
Q_BLOCK = 128
SSD_HEADDIM = 64
SSD_HEADS = D_SSD // SSD_HEADDIM
SSD_GROUPS = 2
SSD_STATE = 128
SSD_CONV = 4
SSD_CHUNK = 128
CONV_DIM = D_SSD + 2 * SSD_GROUPS * SSD_STATE
Q_DIM = N_ATT_HEADS * 2 * ATT_DK
IN_PROJ_DIM = 2 * Q_DIM + D_ATT + D_SSD + CONV_DIM + SSD_HEADS
D_FF = 5632
FFN_CONV = 3
EPS = 1e-6

kernel_name = "hymba_diffattn_mamba2_convffn_step"


def rmsnorm(x, w):
    xf = x.astype(jnp.float32)
    y = xf * lax.rsqrt(jnp.mean(xf * xf, axis=-1, keepdims=True) + EPS)
    return (y * w.astype(jnp.float32)).astype(x.dtype)


def partial_rope(x, pos):
    half = ROT_DIM // 2
    inv = ROPE_THETA ** (-jnp.arange(half, dtype=jnp.float32) / half)
    ang = pos.astype(jnp.float32)[:, None] * inv[None, :]
    cos = jnp.cos(ang)[None, :, None, None, :].astype(x.dtype)
    sin = jnp.sin(ang)[None, :, None, None, :].astype(x.dtype)
    x1 = x[..., :half]
    x2 = x[..., half:ROT_DIM]
    return jnp.concatenate([x1 * cos - x2 * sin, x2 * cos + x1 * sin, x[..., ROT_DIM:]], axis=-1)


def causal_dwconv(x, prev, w, b):
    width = w.shape[0]
    L = x.shape[1]
    xp = jnp.concatenate([prev.astype(x.dtype), x], axis=1)
    out = b
    for j in range(width):
        out = out + xp[:, j:j + L] * w[j]
    return out, xp[:, L:]


def diff_attend(q, k, v, mask, lam):
    s = jnp.einsum("bqhcd,bkhcd->bhcqk", q, k).astype(jnp.float32) * (ATT_DK ** -0.5)
    s = jnp.where(mask[None, None, None], s, -jnp.inf)
    p = jax.nn.softmax(s, axis=-1)
    a = p[:, :, 0] - lam * p[:, :, 1]
    return jnp.einsum("bhqk,bkhd->bqhd", a.astype(v.dtype), v)


def attend_prompt(q, k, v, lam):
    b, s = q.shape[0], q.shape[1]
    nb = s // Q_BLOCK
    qb = q.reshape(b, nb, Q_BLOCK, N_ATT_HEADS, 2, ATT_DK).swapaxes(0, 1)
    kpos = jnp.arange(s)

    def block(args):
        i, qi = args
        qpos = i * Q_BLOCK + jnp.arange(Q_BLOCK)
        return diff_attend(qi, k, v, kpos[None, :] <= qpos[:, None], lam)

    out = lax.map(block, (jnp.arange(nb), qb))
    return out.swapaxes(0, 1).reshape(b, s, N_ATT_HEADS, ATT_DV)


def ssd_scan(x, dt, A, Bm, Cm, h0):
    b, l = x.shape[0], x.shape[1]
    q = SSD_CHUNK if l % SSD_CHUNK == 0 else l
    c = l // q
    R = SSD_HEADS // SSD_GROUPS
    f32 = jnp.float32
    xdt = (x.astype(f32) * dt[..., None]).reshape(b, c, q, SSD_GROUPS, R, SSD_HEADDIM)
    a = (dt * A).reshape(b, c, q, SSD_GROUPS, R)
    Bc = Bm.astype(f32).reshape(b, c, q, SSD_GROUPS, SSD_STATE)
    Cc = Cm.astype(f32).reshape(b, c, q, SSD_GROUPS, SSD_STATE)
    a_cs = jnp.cumsum(a, axis=2)
    seg = a_cs[:, :, :, None] - a_cs[:, :, None, :]
    tri = jnp.tril(jnp.ones((q, q), dtype=bool))[None, None, :, :, None, None]
    Lmat = jnp.exp(jnp.where(tri, seg, -jnp.inf))
    cb = jnp.einsum("bclgn,bcsgn->bclsg", Cc, Bc)
    y_diag = jnp.einsum("bclsg,bclsgr,bcsgrp->bclgrp", cb, Lmat, xdt)
    decay = jnp.exp(a_cs[:, :, -1:] - a_cs)
    chunk_states = jnp.einsum("bcsgn,bcsgr,bcsgrp->bcgrpn", Bc, decay, xdt)
    chunk_decay = jnp.exp(a_cs[:, :, -1])

    def step(h, inp):
        st, dec = inp
        return h * dec[..., None, None] + st, h

    h_init = h0.astype(f32).reshape(b, SSD_GROUPS, R, SSD_HEADDIM, SSD_STATE)
    h_final, h_prev = lax.scan(step, h_init, (chunk_states.swapaxes(0, 1), chunk_decay.swapaxes(0, 1)))
    h_prev = h_prev.swapaxes(0, 1)
    y_off = jnp.einsum("bclgn,bcgrpn,bclgr->bclgrp", Cc, h_prev, jnp.exp(a_cs))
    y = (y_diag + y_off).reshape(b, l, SSD_HEADS, SSD_HEADDIM)
    return y, h_final.reshape(b, SSD_HEADS, SSD_HEADDIM, SSD_STATE)


def hybrid_layer(x, positions, past_k, past_v, conv_ssd_prev, ssm_prev, conv_ffn_prev, lam_init,
                 norm_mix_w, w_in, lambda_q1, lambda_k1, lambda_q2, lambda_k2, subln_w,
                 conv_ssd_w, conv_ssd_b, dt_bias, a_log, d_skip, norm_ssd_w, w_out,
                 norm_ffn_w, w_gate, w_up, conv_ffn_w, conv_ffn_b, w_down):
    b, s = x.shape[0], x.shape[1]
    h = rmsnorm(x, norm_mix_w)
    proj = h @ w_in
    o1 = Q_DIM
    o2 = o1 + Q_DIM
    o3 = o2 + D_ATT
    o4 = o3 + D_SSD
    o5 = o4 + CONV_DIM
    q = partial_rope(proj[..., :o1].reshape(b, s, N_ATT_HEADS, 2, ATT_DK), positions)
    k = partial_rope(proj[..., o1:o2].reshape(b, s, N_ATT_HEADS, 2, ATT_DK), positions)
    v = proj[..., o2:o3].reshape(b, s, N_ATT_HEADS, ATT_DV)
    z = proj[..., o3:o4]
    xbc = proj[..., o4:o5]
    dt_raw = proj[..., o5:]

    f32 = jnp.float32
    lam = (jnp.exp(jnp.sum(lambda_q1.astype(f32) * lambda_k1.astype(f32)))
           - jnp.exp(jnp.sum(lambda_q2.astype(f32) * lambda_k2.astype(f32))) + lam_init)
    if past_k is None:
        att = attend_prompt(q, k, v, lam)
    else:
        past_len = past_k.shape[1]
        k_all = jnp.concatenate([past_k.astype(k.dtype), k], axis=1)
        v_all = jnp.concatenate([past_v.astype(v.dtype), v], axis=1)
        kpos = jnp.arange(past_len + s)
        mask = kpos[None, :] <= positions[:, None]
        att = diff_attend(q, k_all, v_all, mask, lam)
    att = (rmsnorm(att, subln_w) * (1.0 - lam_init)).reshape(b, s, D_ATT)

    xbc_c, new_conv_ssd = causal_dwconv(xbc, conv_ssd_prev, conv_ssd_w, conv_ssd_b)
    xbc_c = jax.nn.silu(xbc_c)
    xs = xbc_c[..., :D_SSD].reshape(b, s, SSD_HEADS, SSD_HEADDIM)
    Bm = xbc_c[..., D_SSD:D_SSD + SSD_GROUPS * SSD_STATE].reshape(b, s, SSD_GROUPS, SSD_STATE)
    Cm = xbc_c[..., D_SSD + SSD_GROUPS * SSD_STATE:].reshape(b, s, SSD_GROUPS, SSD_STATE)
    dt = jax.nn.softplus(dt_raw.astype(f32) + dt_bias.astype(f32))
    A = -jnp.exp(a_log.astype(f32))
    y, ssm_new = ssd_scan(xs, dt, A, Bm, Cm, ssm_prev)
    y = (y + d_skip.astype(f32)[:, None] * xs.astype(f32)).astype(x.dtype).reshape(b, s, D_SSD)
    y = y * jax.nn.silu(z)
    y = rmsnorm(y.reshape(b, s, SSD_GROUPS, D_SSD // SSD_GROUPS),
                norm_ssd_w.reshape(SSD_GROUPS, D_SSD // SSD_GROUPS)).reshape(b, s, D_SSD)

    x = x + jnp.concatenate([att, y], axis=-1) @ w_out

    hf = rmsnorm(x, norm_ffn_w)
    g = hf @ w_gate
    u = hf @ w_up
    g_c, new_conv_ffn = causal_dwconv(g, conv_ffn_prev, conv_ffn_w, conv_ffn_b)
    x = x + (jax.nn.silu(g_c) * u) @ w_down
    return x, k, v, ssm_new.astype(ssm_prev.dtype), new_conv_ssd, new_conv_ffn


def setup_inputs(seed: int = 0) -> dict:
    key = jax.random.key(seed)
    ks = jax.random.split(key, 32)
    f32 = jnp.float32
    n_pages = PAST_LEN // PAGE_SIZE
    used = DEC_BATCH * n_pages
    n_phys = used + max(1, used // 4)
    nrm = lambda k, shape, scale: jax.random.normal(k, shape, f32) * scale
    page_table = jax.random.permutation(ks[7], n_phys)[:used].reshape(DEC_BATCH, n_pages).astype(jnp.int32)
    dt0 = jnp.exp(jax.random.uniform(ks[13], (DEPTH, SSD_HEADS), f32) * (math.log(0.1) - math.log(0.001)) + math.log(0.001))
    return {
        "x_prompt": nrm(ks[0], (BATCH, SEQ, D_MODEL), 1.0),
        "x_sample": nrm(ks[1], (DEC_BATCH, DEC_SEQ, D_MODEL), 1.0),
        "cache_k": nrm(ks[2], (DEPTH, n_phys, PAGE_SIZE, N_ATT_HEADS, 2, ATT_DK), 1.0),
        "cache_v": nrm(ks[3], (DEPTH, n_phys, PAGE_SIZE, N_ATT_HEADS, ATT_DV), 1.0),
        "state_ssm": nrm(ks[4], (DEPTH, DEC_BATCH, SSD_HEADS, SSD_HEADDIM, SSD_STATE), 0.1),
        "state_conv_ssd": nrm(ks[5], (DEPTH, DEC_BATCH, SSD_CONV - 1, CONV_DIM), 1.0),
        "state_conv_ffn": nrm(ks[6], (DEPTH, DEC_BATCH, FFN_CONV - 1, D_FF), 1.0),
        "page_table": page_table,
        "norm_mix_w": 1.0 + nrm(ks[8], (DEPTH, D_MODEL), 0.02),
        "w_in": nrm(ks[9], (DEPTH, D_MODEL, IN_PROJ_DIM), D_MODEL ** -0.5),
        "lambda_q1": nrm(ks[10], (DEPTH, ATT_DK), 0.1),
        "lambda_k1": nrm(ks[11], (DEPTH, ATT_DK), 0.1),
        "lambda_q2": nrm(ks[12], (DEPTH, ATT_DK), 0.1),
        "lambda_k2": nrm(ks[14], (DEPTH, ATT_DK), 0.1),
        "subln_w": 1.0 + nrm(ks[15], (DEPTH, ATT_DV), 0.02),
        "conv_ssd_w": nrm(ks[16], (DEPTH, SSD_CONV, CONV_DIM), SSD_CONV ** -0.5),
        "conv_ssd_b": nrm(ks[17], (DEPTH, CONV_DIM), 0.01),
        "dt_bias": dt0 + jnp.log(-jnp.expm1(-dt0)),
        "a_log": jnp.log(jax.random.uniform(ks[18], (DEPTH, SSD_HEADS), f32, 1.0, 16.0)),
        "d_skip": 1.0 + nrm(ks[19], (DEPTH, SSD_HEADS), 0.02),
        "norm_ssd_w": 1.0 + nrm(ks[20], (DEPTH, D_SSD), 0.02),
        "w_out": nrm(ks[21], (DEPTH, D_MIX, D_MODEL), D_MIX ** -0.5),
        "norm_ffn_w": 1.0 + nrm(ks[22], (DEPTH, D_MODEL), 0.02),
        "w_gate": nrm(ks[23], (DEPTH, D_MODEL, D_FF), D_MODEL ** -0.5),
        "w_up": nrm(ks[24], (DEPTH, D_MODEL, D_FF), D_MODEL ** -0.5),
        "conv_ffn_w": nrm(ks[25], (DEPTH, FFN_CONV, D_FF), FFN_CONV ** -0.5),
        "conv_ffn_b": nrm(ks[26], (DEPTH, D_FF), 0.01),
        "w_down": nrm(ks[27], (DEPTH, D_FF, D_MODEL), D_FF ** -0.5),
        "norm_final_w": 1.0 + nrm(ks[28], (D_MODEL,), 0.02),
    }


def reference(x_prompt, x_sample, cache_k, cache_v, state_ssm, state_conv_ssd, state_conv_ffn, page_table,
              norm_mix_w, w_in, lambda_q1, lambda_k1, lambda_q2, lambda_k2, subln_w,
              conv_ssd_w, conv_ssd_b, dt_bias, a_log, d_skip, norm_ssd_w, w_out,
              norm_ffn_w, w_gate, w_up, conv_ffn_w, conv_ffn_b, w_down, norm_final_w):
    bp, sp = x_prompt.shape[0], x_prompt.shape[1]
    db, ds = x_sample.shape[0], x_sample.shape[1]
    n_pages = page_table.shape[1]
    past_len = n_pages * PAGE_SIZE
    pos_p = jnp.arange(sp)
    pos_s = past_len + jnp.arange(ds)
    xp, xs = x_prompt, x_sample
    kp_l, vp_l, sp_l, cp_l, fp_l = [], [], [], [], []
    ks_l, vs_l, ss_l, cs_l, fs_l = [], [], [], [], []
    for i in range(DEPTH):
        lam_init = 0.8 - 0.6 * math.exp(-0.3 * i)
        params = (norm_mix_w[i], w_in[i], lambda_q1[i], lambda_k1[i], lambda_q2[i], lambda_k2[i], subln_w[i],
                  conv_ssd_w[i], conv_ssd_b[i], dt_bias[i], a_log[i], d_skip[i], norm_ssd_w[i], w_out[i],
                  norm_ffn_w[i], w_gate[i], w_up[i], conv_ffn_w[i], conv_ffn_b[i], w_down[i])
        xp, kp, vp, ssp, cvp, ffp = hybrid_layer(
            xp, pos_p, None, None,
            jnp.zeros((bp, SSD_CONV - 1, CONV_DIM), xp.dtype),
            jnp.zeros((bp, SSD_HEADS, SSD_HEADDIM, SSD_STATE), xp.dtype),
            jnp.zeros((bp, FFN_CONV - 1, D_FF), xp.dtype),
            lam_init, *params)
        past_k = cache_k[i, page_table].reshape(db, past_len, N_ATT_HEADS, 2, ATT_DK)
        past_v = cache_v[i, page_table].reshape(db, past_len, N_ATT_HEADS, ATT_DV)
        xs, ksm, vsm, sss, cvs, ffs = hybrid_layer(
            xs, pos_s, past_k, past_v, state_conv_ssd[i], state_ssm[i], state_conv_ffn[i],
            lam_init, *params)
        kp_l.append(kp); vp_l.append(vp); sp_l.append(ssp); cp_l.append(cvp); fp_l.append(ffp)
        ks_l.append(ksm); vs_l.append(vsm); ss_l.append(sss); cs_l.append(cvs); fs_l.append(ffs)
    y_prompt = rmsnorm(xp, norm_final_w)
    y_sample = rmsnorm(xs, norm_final_w)
    return (y_prompt, y_sample,
            jnp.stack(kp_l), jnp.stack(vp_l), jnp.stack(sp_l), jnp.stack(cp_l), jnp.stack(fp_l),
            jnp.stack(ks_l), jnp.stack(vs_l), jnp.stack(ss_l), jnp.stack(cs_l), jnp.stack(fs_l))
```

```python
import numpy as np
import os
KSTAGE = int(os.environ.get('KSTAGE', '99'))
KSUB = int(os.environ.get('KSUB', '99'))
KQ = int(os.environ.get('KQ', '99'))
ATT = int(os.environ.get('ATT', '99'))
from contextlib import ExitStack
import concourse.bass as bass
import concourse.mybir as mybir
from concourse.bass_utils import run_bass_kernel_spmd

F32 = mybir.dt.float32
BF16 = mybir.dt.bfloat16
I32 = mybir.dt.int32
AF = mybir.ActivationFunctionType
ALU = mybir.AluOpType
AX = mybir.AxisListType

D = 2048
NU = 17
SU = 16
OWN0 = 7
NCOL = 2048 + 64
NOWNCOL = 9 * 128 + 64
DFF = 5632
NFC = DFF // 128
NPHYS_ROWS = int(os.environ.get('KPHYS', '2560')) * 128
EPS = 1e-6
LAM_INIT = 0.8 - 0.6 * 1.0
NEG = -30000.0


def nu(u):
    return 128 if u < 16 else 64


def ocol(u):
    return (u - OWN0) * 128


class Sched:
    def __init__(self, nc, stack):
        self.nc = nc
        self.stack = stack
        self.eng = {'pe': nc.tensor, 'act': nc.scalar, 'dve': nc.vector, 'pool': nc.gpsimd, 'sp': nc.sync}
        self.csem = {e: stack.enter_context(nc.semaphore("cs_" + e)) for e in self.eng}
        self.cnt = {e: 0 for e in self.eng}
        self.waited = {e: {} for e in self.eng}
        self.regions = {}
        self.dsem = {}

    def _tok_wait(self, engine, tok):
        kind, key, val = tok
        sem = self.csem[key] if kind == 'c' else self.dsem[key][0]
        wk = (kind, key)
        if self.waited[engine].get(wk, 0) >= val:
            return
        self.waited[engine][wk] = val
        self.eng[engine].wait_ge(sem, val)

    def _deps(self, engine, reads, writes):
        toks = []
        for r in reads:
            reg = self.regions.get(r)
            if reg and reg['w'] is not None:
                toks.append(reg['w'])
        for w in writes:
            reg = self.regions.get(w)
            if reg:
                if reg['w'] is not None:
                    toks.append(reg['w'])
                toks.extend(reg['r'])
        for t in toks:
            self._tok_wait(engine, t)

    def _register(self, tok, reads, writes):
        for r in reads:
            reg = self.regions.setdefault(r, {'w': None, 'r': []})
            reg['r'].append(tok)
            if len(reg['r']) > 24:
                best = {}
                for t in reg['r']:
                    k = (t[0], t[1])
                    if k not in best or best[k][2] < t[2]:
                        best[k] = t
                reg['r'] = list(best.values())
        for w in writes:
            self.regions[w] = {'w': tok, 'r': []}

    def op(self, engine, fn, reads=(), writes=()):
        self._deps(engine, reads, writes)
        inst = fn(self.eng[engine])
        inst.then_inc(self.csem[engine], 1)
        self.cnt[engine] += 1
        tok = ('c', engine, self.cnt[engine])
        self._register(tok, reads, writes)
        return tok

    def group(self, engine, fns, reads=(), writes=()):
        self._deps(engine, reads, writes)
        inst = None
        for fn in fns:
            inst = fn(self.eng[engine])
        inst.then_inc(self.csem[engine], 1)
        self.cnt[engine] += 1
        tok = ('c', engine, self.cnt[engine])
        self._register(tok, reads, writes)
        return tok

    def dma(self, engine, fn, key, reads=(), writes=()):
        self._deps(engine, reads, writes)
        if key not in self.dsem:
            self.dsem[key] = [self.stack.enter_context(self.nc.semaphore("ds_%d" % len(self.dsem))), 0]
        inst = fn(self.eng[engine])
        self.dsem[key][1] += 16
        inst.then_inc(self.dsem[key][0], 16)
        tok = ('d', key, self.dsem[key][1])
        self._register(tok, reads, writes)
        return tok

    def barrier(self, engines=None):
        for e in (engines or self.eng):
            for key, (sem, val) in self.dsem.items():
                if val:
                    self._tok_wait(e, ('d', key, val))
            for e2 in self.eng:
                if self.cnt[e2]:
                    self._tok_wait(e, ('c', e2, self.cnt[e2]))
        self.regions = {}


def mm(out, lhsT, rhs, start, stop):
    return lambda e: e.matmul(out, lhsT=lhsT, rhs=rhs, start=start, stop=stop)


def tp(out, in_, ident):
    return lambda e: e.transpose(out=out, in_=in_, identity=ident)


def build_program():
    nc = bass.Bass("TRN2", target_bir_lowering=False)
    din = lambda name, shape, dt=F32: nc.dram_tensor(name, list(shape), dt, kind="ExternalInput").ap()
    dout = lambda name, shape, dt=F32: nc.dram_tensor(name, list(shape), dt, kind="ExternalOutput").ap()
    dscr = lambda name, shape, dt=F32: nc.dram_tensor(name, list(shape), dt, kind="Internal").ap()

    xall = din("xall", [NCOL, D])
    ropec = din("ropec", [128, NU, 8])
    ropes = din("ropes", [128, NU, 8])
    validd = din("valid", [128, 1])
    ptb = din("ptb", [128, 256], I32)
    iotad = din("iota", [128, 1])
    identd = din("identd", [128, 128])
    triud = din("triud", [128, 128])
    nmaskd = din("nmaskd", [128, 128])
    maskn4d = din("maskn4d", [4, 4])
    CK = din("cache_k", [NPHYS_ROWS, 1024])
    CV = din("cache_v", [NPHYS_ROWS, 1024])
    st_ssm = din("st_ssm", [16, 16, 64, 128])
    st_cssd = din("st_cssd", [48, 1536])
    st_cffn = din("st_cffn", [32, DFF])
    w_in = din("w_in", [D, 5648])
    w_out = din("w_out", [D, D])
    w_gate = din("w_gate", [D, DFF])
    w_up = din("w_up", [D, DFF])
    w_down = din("w_down", [DFF, D])
    nmw = din("nmw", [128, D])
    nfw = din("nfw", [128, D])
    nlw = din("nlw", [128, D])
    nsw = din("nsw", [128, 1024])
    sublw = din("sublw", [128, 128])
    lamd = din("lamd", [128, 4, 64])
    dtbd = din("dtbd", [128, 16])
    alogd = din("alogd", [128, 16])
    dskd = din("dskd", [128, 16])
    cswd = din("cswd", [128, 12, 4])
    csbd = din("csbd", [128, 12])
    cfwd = din("cfwd", [128, NFC, 3])
    cfbd = din("cfbd", [128, NFC])

    YP = dout("YP", [1024, D])
    KP = dout("KP", [1024, 1024])
    VP = dout("VP", [1024, 1024])
    SSMP = dout("SSMP", [16, 64, 128])
    CSSDP = dout("CSSDP", [3, 1536])
    CFFNP = dout("CFFNP", [2, DFF])
    YS = dout("YS", [64, D])
    KS = dout("KS", [64, 1024])
    VS = dout("VS", [64, 1024])
    SSMS = dout("SSMS", [16, 16, 64, 128])
    CSSDS = dout("CSSDS", [16, 3, 1536])
    CFFNS = dout("CFFNS", [16, 2, DFF])

    att_scr = dscr("att_scr", [64, 1024], BF16)
    xbcs_scr = dscr("xbcs_scr", [64, 1280], BF16)
    dts_scr = dscr("dts_scr", [64, 16])
    ys_scr = dscr("ys_scr", [64, 1024])

    with ExitStack() as st:
        S = Sched(nc, st)
        op, grp, dma = S.op, S.group, S.dma

        def sb(stack, name, shape, dt):
            return stack.enter_context(nc.sbuf_tensor(name, list(shape), dt))

        pall = st.enter_context(nc.psum_tensor("pall", [128, 8, 512], F32))
        pb = [pall[:, i, :] for i in range(8)]
        pallh = pall[:].rearrange("p a b -> p (a b)").bitcast(BF16)
        pbh = [pallh[:, i * 1024:(i + 1) * 1024] for i in range(8)]
        pctr = [0]

        def nextpb(lo=0, hi=8):
            i = lo + pctr[0] % (hi - lo)
            pctr[0] += 1
            return i

        identf = sb(st, "identf", [128, 128], F32)
        ident = sb(st, "ident", [128, 128], BF16)
        triu = sb(st, "triu", [128, 128], F32)
        nmaskf = sb(st, "nmaskf", [128, 128], F32)
        nmask = sb(st, "nmask", [128, 128], BF16)
        onesf = sb(st, "onesf", [128, 128], F32)
        valid = sb(st, "validt", [128, 1], F32)
        vbias = sb(st, "vbias", [128, 1], F32)
        mk4s = sb(st, "mk4s", [4, 4], F32)
        dma('sp', lambda e: e.dma_start(out=mk4s[:], in_=maskn4d), key='c4', writes=['mk4s'])
        junk = sb(st, "junk", [128, 2048], BF16)
        dma('sp', lambda e: e.dma_start(out=identf[:], in_=identd), key='c0', writes=['identf'])
        dma('sp', lambda e: e.dma_start(out=triu[:], in_=triud), key='c1', writes=['triu'])
        dma('sp', lambda e: e.dma_start(out=nmaskf[:], in_=nmaskd), key='c2', writes=['nmaskf'])
        dma('sp', lambda e: e.dma_start(out=valid[:], in_=validd), key='c3', writes=['valid'])
        op('dve', lambda e: e.tensor_copy(out=ident[:], in_=identf[:]), reads=['identf'], writes=['ident'])
        op('dve', lambda e: e.tensor_copy(out=nmask[:], in_=nmaskf[:]), reads=['nmaskf'], writes=['nmask'])
        op('pool', lambda e: e.memset(onesf[:], 1.0), writes=['onesf'])
        op('dve', lambda e: e.tensor_scalar(out=vbias[:], in0=valid[:], scalar1=-NEG, scalar2=NEG, op0=ALU.mult, op1=ALU.add),
           reads=['valid'], writes=['vbias'])

        def rstd_from_ss(ss_ap, out_ap, tmp_ap, n_elem, rkey, wkey):
            op('dve', lambda e: e.tensor_scalar(out=tmp_ap, in0=ss_ap, scalar1=1.0 / n_elem, scalar2=EPS, op0=ALU.mult, op1=ALU.add),
               reads=[rkey], writes=[wkey + 't'])
            op('act', lambda e: e.activation(out=tmp_ap, in_=tmp_ap, func=AF.Sqrt), reads=[wkey + 't'], writes=[wkey + 't'])
            op('dve', lambda e: e.reciprocal(out=out_ap, in_=tmp_ap), reads=[wkey + 't'], writes=[wkey])

        evac_ctr = [0]

        def evac(out_ap, in_ap, reads, writes, eng=None):
            if eng is None:
                eng = 'act' if evac_ctr[0] % 2 == 0 else 'dve'
                evac_ctr[0] += 1
            if eng == 'act':
                return op('act', lambda e: e.copy(out=out_ap, in_=in_ap), reads=reads, writes=writes)
            return op(eng, lambda e: e.tensor_copy(out=out_ap, in_=in_ap), reads=reads, writes=writes)

        attT = sb(st, "attT", [128, 8, NOWNCOL], BF16)

        p12 = ExitStack()
        hT = sb(p12, "hT", [128, 16, NCOL], BF16)
        with ExitStack() as p1:
            nmw_t = sb(p1, "nmw_t", [128, D], F32)
            xbuf = [sb(p1, "xbuf%d" % i, [128, D], F32) for i in range(2)]
            xbb = [sb(p1, "xbb%d" % i, [128, D], BF16) for i in range(2)]
            ss1 = sb(p1, "ss1", [128, 3 * NU], F32)
            dma('sp', lambda e: e.dma_start(out=nmw_t[:], in_=nmw), key='nmw', writes=['nmw'])
            op('pool', lambda e: e.memset(ss1[:], 0.0), writes=['ss1'])
            for u in [int(x) for x in os.environ.get('KU', ','.join(str(i) for i in range(NU))).split(',')]:
                n = nu(u)
                xt, xb_, k = xbuf[u % 2], xbb[u % 2], u % 2
                dma('sp', lambda e: e.dma_start(out=xt[:n, :], in_=xall[u * 128:u * 128 + n, :]), key='xb%d' % k, writes=['xbuf%d' % k])
                op('act', lambda e: e.activation(out=junk[:n, :], in_=xt[:n, :], func=AF.Square, accum_out=ss1[:n, u:u + 1]),
                   reads=['xbuf%d' % k, 'ss1'], writes=['junk', 'ss1a%d' % u])
                rstd_from_ss(ss1[:n, u:u + 1], ss1[:n, 2 * NU + u:2 * NU + u + 1], ss1[:n, NU + u:NU + u + 1], D, 'ss1a%d' % u, 'ss1r%d' % u)
                op('dve', lambda e: e.scalar_tensor_tensor(out=xb_[:n, :], in0=xt[:n, :], scalar=ss1[:n, 2 * NU + u:2 * NU + u + 1], in1=nmw_t[:n, :],
                                                          op0=ALU.mult, op1=ALU.mult),
                   reads=['xbuf%d' % k, 'ss1r%d' % u, 'nmw'], writes=['xbb%d' % k])
                for half in range(2):
                    pi = nextpb(0, 4)
                    grp('pe', [tp(pbh[pi][:, j * 128:j * 128 + n], xb_[:n, (half * 8 + j) * 128:(half * 8 + j + 1) * 128], ident[:n, :n]) for j in range(8)],
                        reads=['xbb%d' % k, 'ident'], writes=['pb%d' % pi])
                    evac(hT[:, half * 8:half * 8 + 8, u * 128:u * 128 + n],
                         pbh[pi][:, :].rearrange("p (j t) -> p j t", j=8)[:, :, :n],
                         reads=['pb%d' % pi], writes=['hT%d_%d' % (u, half)])
            S.barrier()
        hT_keys = lambda u: ['hT%d_0' % u, 'hT%d_1' % u]

        for p2 in ([ExitStack()] if KSTAGE >= 1 else []):
            wbuf = [sb(p2, "wbuf%d" % i, [128, 16, 512], BF16) for i in range(1)]
            wctr = [0]
            QT = sb(p2, "QT", [128, 4, NOWNCOL], BF16)
            KT = sb(p2, "KT", [128, 4, NCOL], BF16)
            VA = sb(p2, "VA", [128, NU, 4, 130], BF16)
            ropc = sb(p2, "ropc", [128, NU, 8], F32)
            rops = sb(p2, "rops", [128, NU, 8], F32)
            qk = [sb(p2, "qk%d" % i, [128, 512], F32) for i in range(2)]
            qkb = [sb(p2, "qkb%d" % i, [128, 512], BF16) for i in range(2)]
            rt = sb(p2, "rt", [128, 4, 64], F32)
            sqj = sb(p2, "sqj", [128, 512], F32)
            nrm = sb(p2, "nrm", [128, 32], F32)
            negC = sb(p2, "negC", [128, 2], F32)
            lam_t = sb(p2, "lam_t", [128, 4, 64], F32)
            lamw = sb(p2, "lamw", [128, 8], F32)
            subw = sb(p2, "subw", [128, 128], F32)
            PT = [sb(p2, "PT%d" % i, [128, 256], BF16) for i in range(2)]
            attb = [sb(p2, "attb%d" % i, [128, 512], BF16) for i in range(2)]
            fin = sb(p2, "fin", [128, 16], F32)
            tA = sb(p2, "tA", [128, 128], F32)
            tB = sb(p2, "tB", [128, 128], F32)
            idx = sb(p2, "idx", [128, 2, 256], I32)
            idxf = sb(p2, "idxf", [128, 256], F32)
            iot = sb(p2, "iot", [128, 1], F32)
            ptt = sb(p2, "ptt", [128, 256], I32)
            mk4 = sb(p2, "mk4", [4, 4], F32)
            kpg = [sb(p2, "kpg%d" % i, [128, 512], BF16) for i in range(2)]
            KTs = [sb(p2, "KTs%d" % i, [128, 4, 128], BF16) for i in range(2)]
            vpg = [sb(p2, "vpg%d" % i, [128, 512], BF16) for i in range(2)]
            vnst = sb(p2, "vnst", [4, 512], BF16)
            Vs = [sb(p2, "Vs%d" % i, [128, 16, 4, 130], BF16) for i in range(1)]
            Vn = [sb(p2, "Vn%d" % i, [4, 4, 130], BF16) for i in range(2)]
            Sall = sb(p2, "Sall", [128, 17, 32], F32)
            Pall = sb(p2, "Pall", [128, 17, 32], BF16)
            cm = sb(p2, "cm", [128, 32], F32)
            mx = sb(p2, "mx", [32, 1], F32)
            dg = sb(p2, "dg", [32, 32], F32)
            mxb = sb(p2, "mxb", [128, 32], F32)
            osb = sb(p2, "osb", [4, 8, 132], F32)
            ats = sb(p2, "ats", [4, 4, 128], F32)
            atsb = [sb(p2, "atsb%d" % i, [4, 4, 128], BF16) for i in range(1)]
            sfin = sb(p2, "sfin", [4, 32], F32)

            dma('sp', lambda e: e.dma_start(out=ropc[:], in_=ropec), key='c0', writes=['ropc'])
            dma('sp', lambda e: e.dma_start(out=rops[:], in_=ropes), key='c1', writes=['rops'])
            dma('sp', lambda e: e.dma_start(out=lam_t[:], in_=lamd), key='c2', writes=['lam_t'])
            dma('sp', lambda e: e.dma_start(out=subw[:], in_=sublw), key='c3', writes=['subw'])
            dma('sp', lambda e: e.dma_start(out=ptt[:], in_=ptb), key='c4', writes=['ptt'])
            dma('sp', lambda e: e.dma_start(out=iot[:], in_=iotad), key='c5', writes=['iot'])
            dma('sp', lambda e: e.dma_start(out=mk4[:], in_=maskn4d), key='c6', writes=['mk4'])
            op('dve', lambda e: e.tensor_tensor(out=rt[:, 0, :], in0=lam_t[:, 0, :], in1=lam_t[:, 1, :], op=ALU.mult), reads=['lam_t'], writes=['rt'])
            op('dve', lambda e: e.tensor_reduce(out=lamw[:, 0:1], in_=rt[:, 0, :], axis=AX.X, op=ALU.add), reads=['rt'], writes=['lamw0'])
            op('dve', lambda e: e.tensor_tensor(out=rt[:, 1, :], in0=lam_t[:, 2, :], in1=lam_t[:, 3, :], op=ALU.mult), reads=['lam_t', 'rt'], writes=['rt'])
            op('dve', lambda e: e.tensor_reduce(out=lamw[:, 1:2], in_=rt[:, 1, :], axis=AX.X, op=ALU.add), reads=['rt'], writes=['lamw1'])
            op('act', lambda e: e.activation(out=lamw[:, 0:2], in_=lamw[:, 0:2], func=AF.Exp), reads=['lamw0', 'lamw1'], writes=['lamw01'])
            op('dve', lambda e: e.tensor_tensor(out=lamw[:, 2:3], in0=lamw[:, 1:2], in1=lamw[:, 0:1], op=ALU.subtract), reads=['lamw01'], writes=['lamw2'])
            op('dve', lambda e: e.tensor_scalar(out=lamw[:, 2:3], in0=lamw[:, 2:3], scalar1=-LAM_INIT, scalar2=None, op0=ALU.add), reads=['lamw2'], writes=['neglam'])
            op('dve', lambda e: e.tensor_scalar(out=subw[:], in0=subw[:], scalar1=1.0 - LAM_INIT, scalar2=None, op0=ALU.mult), reads=['subw'], writes=['subw'])
            op('dve', lambda e: e.tensor_copy(out=idxf[:], in_=ptt[:]), reads=['ptt'], writes=['idxf'])
            op('dve', lambda e: e.tensor_scalar(out=idxf[:], in0=idxf[:], scalar1=128.0, scalar2=iot[:, 0:1], op0=ALU.mult, op1=ALU.add),
               reads=['idxf', 'iot'], writes=['idxf'])
            op('dve', lambda e: e.tensor_scalar(out=idxf[:], in0=idxf[:], scalar1=2.0, scalar2=None, op0=ALU.mult), reads=['idxf'], writes=['idxf'])
            op('dve', lambda e: e.tensor_copy(out=idx[:, 0, :], in_=idxf[:]), reads=['idxf'], writes=['idx'])
            op('dve', lambda e: e.tensor_scalar(out=idxf[:], in0=idxf[:], scalar1=1.0, scalar2=None, op0=ALU.add), reads=['idxf', 'idx'], writes=['idxf'])
            op('dve', lambda e: e.tensor_copy(out=idx[:, 1, :], in_=idxf[:]), reads=['idxf'], writes=['idx'])
            op('pool', lambda e: e.memset(Vs[0][:, :, :, 128:129], 1.0), writes=['Vs0'])
            op('pool', lambda e: e.memset(VA[:, :, :, 128:129], 1.0), writes=['VA1'])
            for i in range(2):
                op('pool', lambda e: e.memset(Vn[i][:, :, 128:129], 1.0), writes=['Vn1'])

            def load_w(src_ap, ncols):
                k = wctr[0] % len(wbuf)
                wctr[0] += 1
                dma('pool', lambda e: e.dma_start(out=wbuf[k][:, :, :ncols], in_=src_ap.rearrange("(kc p) n -> p kc n", p=128)),
                    key='w%d' % k, writes=['w%d' % k])
                return wbuf[k], 'w%d' % k

            def inproj_tok(col0, ncols, units, consume):
                wt, wkey = load_w(w_in[:, col0:col0 + ncols], ncols)
                for u in units:
                    n = nu(u)
                    pi = nextpb(0, 4)
                    grp('pe', [mm(pb[pi][:n, :ncols], hT[:, kc, u * 128:u * 128 + n], wt[:, kc, :ncols], kc == 0, kc == 15) for kc in range(16)],
                        reads=hT_keys(u) + [wkey], writes=['pb%d' % pi])
                    consume(u, n, pi)

            for hh in range(2):
                op('pool', lambda e: e.memset(nrm[:], 0.0), reads=[], writes=['nrm', 'qmax', 'kmax'])

                def qk_consume(is_q):
                    def f(u, n, pi):
                        k = u % 2
                        t, tb = qk[k], qkb[k]
                        evac(t[:n, :], pb[pi][:n, :512], reads=['pb%d' % pi], writes=['qk%d' % k], eng='act')
                        if KQ < 1:
                            return
                        v = t[:n, :].rearrange("p (g d) -> p g d", d=64)
                        x1, x2 = v[:, :, 0:8], v[:, :, 8:16]
                        cs = ropc[:n, u, :].unsqueeze(1).to_broadcast([n, 8, 8])
                        sn = rops[:n, u, :].unsqueeze(1).to_broadcast([n, 8, 8])
                        r = rt[:n, :, :].rearrange("p a (g d) -> p a g d", d=8)
                        rk = ['qk%d' % k, 'ropc', 'rops']
                        op('dve', lambda e: e.tensor_tensor(out=r[:, 0], in0=x1, in1=cs, op=ALU.mult), reads=rk, writes=['rt'])
                        op('dve', lambda e: e.tensor_tensor(out=r[:, 1], in0=x2, in1=sn, op=ALU.mult), reads=rk + ['rt'], writes=['rt'])
                        op('dve', lambda e: e.tensor_tensor(out=r[:, 2], in0=x1, in1=sn, op=ALU.mult), reads=rk + ['rt'], writes=['rt'])
                        op('dve', lambda e: e.tensor_tensor(out=r[:, 3], in0=x2, in1=cs, op=ALU.mult), reads=rk + ['rt'], writes=['rt'])
                        op('dve', lambda e: e.tensor_tensor(out=x1, in0=r[:, 0], in1=r[:, 1], op=ALU.subtract), reads=['rt', 'qk%d' % k], writes=['qk%d' % k])
                        op('dve', lambda e: e.tensor_tensor(out=x2, in0=r[:, 2], in1=r[:, 3], op=ALU.add), reads=['rt', 'qk%d' % k], writes=['qk%d' % k])
                        if KQ < 2:
                            return
                        mkey = 'qmax' if is_q else 'kmax'
                        mo = 8 if is_q else 16
                        op('dve', lambda e: e.tensor_tensor(out=sqj[:n, :], in0=t[:n, :], in1=t[:n, :], op=ALU.mult), reads=['qk%d' % k], writes=['sqj'])
                        op('dve', lambda e: e.tensor_reduce(out=nrm[:n, 0:8], in_=sqj[:n, :].rearrange("p (g d) -> p g d", d=64), axis=AX.X, op=ALU.add),
                           reads=['sqj'], writes=['nrmr'])
                        op('dve', lambda e: e.tensor_tensor(out=nrm[:n, mo:mo + 8], in0=nrm[:n, mo:mo + 8], in1=nrm[:n, 0:8], op=ALU.max),
                           reads=['nrmr', mkey], writes=[mkey])
                        if (not is_q) and u >= 8:
                            if u < 16:
                                dst = KP[(u - 8) * 128:(u - 8) * 128 + n, hh * 512:(hh + 1) * 512]
                            else:
                                dst = KS[:, hh * 512:(hh + 1) * 512]
                            dma('sp', lambda e: e.dma_start(out=dst, in_=t[:n, :]), key='qk%d' % k, reads=['qk%d' % k])
                        if KQ < 3:
                            return
                        op('act', lambda e: e.copy(out=tb[:n, :], in_=t[:n, :]), reads=['qk%d' % k], writes=['qkb%d' % k])
                        if KQ < 4:
                            return
                        pj = nextpb(int(os.environ.get("PJLO", "4")), int(os.environ.get("PJHI", "6")))
                        grp('pe', [tp(pbh[pj][:, h4 * 128:h4 * 128 + n], tb[:n, h4 * 128:(h4 + 1) * 128], ident[:n, :n]) for h4 in range(4)],
                            reads=['qkb%d' % k, 'ident'], writes=['pb%d' % pj])
                        if KQ < 5:
                            return
                        src = pbh[pj][:, 0:512].rearrange("p (j t) -> p j t", j=4)[:, :, :n]
                        if is_q:
                            evac(QT[:, :, ocol(u):ocol(u) + n], src, reads=['pb%d' % pj], writes=['QT%d' % u], eng='dve')
                        else:
                            evac(KT[:, :, u * 128:u * 128 + n], src, reads=['pb%d' % pj], writes=['KT%d' % u], eng='dve')
                    return f

                def v_consume(u, n, pi):
                    k = u % 2
                    evac(VA[:n, u, :, 0:128], pb[pi][:n, :512].rearrange("p (h d) -> p h d", d=128), reads=['pb%d' % pi], writes=['VA%d' % u], eng='act')
                    if u >= 8:
                        t = qk[k]
                        evac(t[:n, :], pb[pi][:n, :512], reads=['pb%d' % pi], writes=['qk%d' % k], eng='act')
                        if u < 16:
                            dst = VP[(u - 8) * 128:(u - 8) * 128 + n, hh * 512:(hh + 1) * 512]
                        else:
                            dst = VS[:, hh * 512:(hh + 1) * 512]
                        dma('sp', lambda e: e.dma_start(out=dst, in_=t[:n, :]), key='qk%d' % k, reads=['qk%d' % k], writes=['VSd'] if u == 16 else [])

                if KSUB >= 1:
                    inproj_tok(hh * 512, 512, list(range(OWN0, NU)), qk_consume(True))
                if KSUB >= 2:
                    inproj_tok(1024 + hh * 512, 512, list(range(NU)), qk_consume(False))
                if KSUB >= 3:
                    inproj_tok(2048 + hh * 512, 512, list(range(NU)), v_consume)
                if KSUB < 4:
                    continue

                op('dve', lambda e: e.tensor_reduce(out=nrm[:, 24:25], in_=nrm[:, 8:16], axis=AX.X, op=ALU.max), reads=['qmax'], writes=['n24'])
                op('dve', lambda e: e.tensor_reduce(out=nrm[:, 25:26], in_=nrm[:, 16:24], axis=AX.X, op=ALU.max), reads=['kmax'], writes=['n25'])
                pi = nextpb(0, 4)
                grp('pe', [tp(pb[pi][0:2, 0:128], nrm[:, 24:26], identf[:, :])], reads=['n24', 'n25', 'identf'], writes=['pb%d' % pi])
                op('dve', lambda e: e.tensor_reduce(out=nrm[0:2, 26:27], in_=pb[pi][0:2, 0:128], axis=AX.X, op=ALU.max), reads=['pb%d' % pi], writes=['n26'])
                op('dve', lambda e: e.tensor_scalar(out=nrm[0:2, 28:30], in0=identf[0:2, 0:2], scalar1=nrm[0:2, 26:27], scalar2=None, op0=ALU.mult),
                   reads=['n26', 'identf'], writes=['n28'])
                pi = nextpb(0, 4)
                grp('pe', [mm(pb[pi][:, 0:2], onesf[0:2, :], nrm[0:2, 28:30], True, True)], reads=['n28', 'onesf'], writes=['pb%d' % pi])
                op('dve', lambda e: e.tensor_copy(out=nrm[:, 30:32], in_=pb[pi][:, 0:2]), reads=['pb%d' % pi], writes=['n30'])
                op('dve', lambda e: e.tensor_tensor(out=nrm[:, 27:28], in0=nrm[:, 30:31], in1=nrm[:, 31:32], op=ALU.mult), reads=['n30'], writes=['n27'])
                op('act', lambda e: e.activation(out=nrm[:, 27:28], in_=nrm[:, 27:28], func=AF.Sqrt), reads=['n27'], writes=['n27'])
                op('dve', lambda e: e.tensor_scalar(out=negC[:, 0:1], in0=nrm[:, 27:28], scalar1=-0.125, scalar2=None, op0=ALU.mult), reads=['n27'], writes=['negC0'])
                op('dve', lambda e: e.tensor_tensor(out=negC[:, 1:2], in0=negC[:, 0:1], in1=vbias[:, 0:1], op=ALU.add), reads=['negC0', 'vbias'], writes=['negC1'])

                fctr = 0
                for i in range(OWN0, 16 if KSTAGE >= 2 else OWN0):
                    ab = attb[i % 2]
                    for h4 in range(4):
                        po0, po1 = (4, 5) if (fctr % 2 == 0) else (6, 7)
                        fctr += 1
                        for j in range(i + 1):
                            pi = 2 * nextpb(0, 2)
                            diag = (j == i)
                            fns = []
                            for c in range(2):
                                fns.append(mm(pb[pi + c][:, 0:128], KT[c * 64:(c + 1) * 64, h4, j * 128:(j + 1) * 128],
                                              QT[c * 64:(c + 1) * 64, h4, ocol(i):ocol(i) + 128], True, not diag))
                                if diag:
                                    fns.append(mm(pb[pi + c][:, 0:128], ident[:, :], nmask[:, :], False, True))
                            grp('pe', fns, reads=['KT%d' % j, 'QT%d' % i, 'ident', 'nmask'], writes=['pb%d' % pi, 'pb%d' % (pi + 1)])
                            pt = PT[j % 2]
                            bias = negC[:, 1:2] if (j < 8 and not (i == OWN0 and j == OWN0)) else negC[:, 0:1]
                            op('act', lambda e: e.activation(out=pt[:, :].rearrange("p (c q) -> p c q", c=2), in_=pall[:, pi:pi + 2, 0:128], func=AF.Exp, bias=bias, scale=0.125),
                               reads=['pb%d' % pi, 'pb%d' % (pi + 1), 'negC0', 'negC1'], writes=['PT%d' % (j % 2)])
                            if ATT < 2:
                                continue
                            grp('pe', [mm(pb[po0][:, 0:129], pt[:, 0:128], VA[:, j, h4, 0:129], j == 0, j == i),
                                       mm(pb[po1][:, 0:129], pt[:, 128:256], VA[:, j, h4, 0:129], j == 0, j == i)],
                                reads=['PT%d' % (j % 2), 'VA%d' % j, 'VA1'], writes=['pb%d' % po0, 'pb%d' % po1])
                        if ATT < 3:
                            continue
                        rk = ['pb%d' % po0, 'pb%d' % po1]
                        op('dve', lambda e: e.reciprocal(out=fin[:, 0:1], in_=pb[po0][:, 128:129]), reads=rk, writes=['fin0'])
                        op('dve', lambda e: e.reciprocal(out=fin[:, 1:2], in_=pb[po1][:, 128:129]), reads=rk, writes=['fin1'])
                        op('dve', lambda e: e.tensor_tensor(out=fin[:, 2:3], in0=fin[:, 1:2], in1=lamw[:, 2:3], op=ALU.mult), reads=['fin1', 'neglam'], writes=['fin2'])
                        op('dve', lambda e: e.tensor_scalar(out=tA[:, :], in0=pb[po0][:, 0:128], scalar1=fin[:, 0:1], scalar2=None, op0=ALU.mult),
                           reads=rk + ['fin0'], writes=['tA'])
                        op('dve', lambda e: e.scalar_tensor_tensor(out=tB[:, :], in0=pb[po1][:, 0:128], scalar=fin[:, 2:3], in1=tA[:, :], op0=ALU.mult, op1=ALU.add),
                           reads=rk + ['fin2', 'tA'], writes=['tB'])
                        op('dve', lambda e: e.memset(fin[:, 3:4], 0.0), writes=['fin3'])
                        op('act', lambda e: e.activation(out=junk[:, 0:128], in_=tB[:, :], func=AF.Square, accum_out=fin[:, 3:4]),
                           reads=['tB', 'fin3'], writes=['junk', 'fin3a'])
                        rstd_from_ss(fin[:, 3:4], fin[:, 5:6], fin[:, 4:5], 128, 'fin3a', 'fin5')
                        op('dve', lambda e: e.scalar_tensor_tensor(out=ab[:, h4 * 128:(h4 + 1) * 128], in0=tB[:, :], scalar=fin[:, 5:6], in1=subw[:, :],
                                                                  op0=ALU.mult, op1=ALU.mult),
                           reads=['tB', 'fin5', 'subw'], writes=['attb%d' % (i % 2)])
                    if ATT < 4:
                        continue
                    pj = nextpb(0, 4)
                    grp('pe', [tp(pbh[pj][:, h4 * 128:(h4 + 1) * 128], ab[:, h4 * 128:(h4 + 1) * 128], ident[:, :]) for h4 in range(4)],
                        reads=['attb%d' % (i % 2), 'ident'], writes=['pb%d' % pj])
                    evac(attT[:, hh * 4:hh * 4 + 4, ocol(i):ocol(i) + 128], pbh[pj][:, 0:512].rearrange("p (j t) -> p j t", j=4),
                         reads=['pb%d' % pj], writes=['attT%d_%d' % (i, hh)])

                for b in range(16 if KSTAGE >= 3 else 0):
                    vs = Vs[0]
                    vkey = 'Vs0'
                    vn = Vn[b % 2]
                    dma('pool', lambda e: e.dma_start(out=vnst[:, :], in_=VS[b * 4:b * 4 + 4, hh * 512:(hh + 1) * 512]),
                        key='vnst', reads=['VSd'], writes=['vnst'])
                    op('act', lambda e: e.copy(out=vn[:, :, 0:128], in_=vnst[:, :].rearrange("p (h d) -> p h d", d=128)),
                       reads=['vnst'], writes=['Vn%d' % (b % 2)])
                    op('pool', lambda e: e.memset(Sall[:, 16, :], NEG), writes=['Sall'])
                    pS = 0
                    first = True
                    for j in range(16):
                        col = b * 16 + j
                        kp, kk = kpg[j % 2], j % 2
                        dma('pool', lambda e: e.indirect_dma_start(out=kp[:, :], out_offset=None, in_=CK.rearrange("r (a c) -> (r a) c", a=2),
                                                                  in_offset=bass.IndirectOffsetOnAxis(ap=idx[:, hh, col:col + 1], axis=0)),
                            key='kpg%d' % kk, reads=['idx'], writes=['kpg%d' % kk])
                        vp = vpg[kk]
                        dma('pool', lambda e: e.indirect_dma_start(out=vp[:, :], out_offset=None,
                                                                  in_=CV.rearrange("r (a c) -> (r a) c", a=2),
                                                                  in_offset=bass.IndirectOffsetOnAxis(ap=idx[:, hh, col:col + 1], axis=0)),
                            key='vpg%d' % kk, reads=['idx'], writes=['vpg%d' % kk])
                        op('act', lambda e: e.copy(out=vs[:, j, :, 0:128], in_=vp[:, :].rearrange("p (h d) -> p h d", d=128)),
                           reads=['vpg%d' % kk], writes=['%s_%d' % (vkey, j)])
                        pj = nextpb(2, 4)
                        grp('pe', [tp(pbh[pj][:, h4 * 128:(h4 + 1) * 128], kp[:, h4 * 128:(h4 + 1) * 128], ident[:, :]) for h4 in range(4)],
                            reads=['kpg%d' % kk, 'ident'], writes=['pb%d' % pj])
                        kts = KTs[kk]
                        evac(kts[:, :, :], pbh[pj][:, 0:512].rearrange("p (j t) -> p j t", j=4), reads=['pb%d' % pj], writes=['KTs%d' % kk])
                        fns = []
                        for h4 in range(4):
                            for c in range(2):
                                s = h4 * 2 + c
                                fns.append(mm(pb[pS + c][:, j * 16 + h4 * 4:j * 16 + h4 * 4 + 4], kts[c * 64:(c + 1) * 64, h4, :],
                                              QT[c * 64:(c + 1) * 64, h4, ocol(SU) + b * 4:ocol(SU) + b * 4 + 4], True, True))
                        grp('pe', fns, reads=['KTs%d' % kk, 'QT%d' % SU], writes=['pb0', 'pb1'] if first else [], )
                        first = False
                    pN = 2
                    fns = []
                    for h4 in range(4):
                        for c in range(2):
                            fns.append(mm(pb[pN + c][0:4, h4 * 4:h4 * 4 + 4], KT[c * 64:(c + 1) * 64, h4, 2048 + b * 4:2048 + b * 4 + 4],
                                          QT[c * 64:(c + 1) * 64, h4, ocol(SU) + b * 4:ocol(SU) + b * 4 + 4], True, True))
                    grp('pe', fns, reads=['KT%d' % SU, 'QT%d' % SU], writes=['pb0', 'pb1', 'pb2', 'pb3'])
                    Sv = Sall[:, :, :].rearrange("p j (h c t) -> p j h c t", c=2, t=4)
                    for c in range(2):
                        evac(Sv[:, 0:16, :, c, :], pb[pS + c][:, 0:256].rearrange("p (j h t) -> p j h t", h=4, t=4),
                             reads=['pb%d' % (pS + c)], writes=['Sall'], eng='act')
                        op('dve', lambda e: e.tensor_tensor(out=Sv[0:4, 16, :, c, :], in0=pb[pN + c][0:4, 0:16].rearrange("p (h t) -> p h t", t=4),
                                                            in1=mk4[:, :].unsqueeze(1).to_broadcast([4, 4, 4]), op=ALU.add),
                           reads=['pb%d' % (pN + c), 'mk4', 'Sall'], writes=['Sall'])
                    op('dve', lambda e: e.tensor_reduce(out=cm[:, :], in_=Sall[:, :, :].rearrange("p j s -> p s j"), axis=AX.X, op=ALU.max),
                       reads=['Sall'], writes=['cm'])
                    pM = nextpb(2, 4)
                    grp('pe', [tp(pb[pM][0:32, 0:128], cm[:, :], identf[:, :])], reads=['cm', 'identf'], writes=['pb%d' % pM])
                    op('dve', lambda e: e.tensor_reduce(out=mx[:, :], in_=pb[pM][0:32, 0:128], axis=AX.X, op=ALU.max), reads=['pb%d' % pM], writes=['mx'])
                    op('dve', lambda e: e.tensor_scalar(out=dg[:, :], in0=identf[0:32, 0:32], scalar1=mx[:, 0:1], scalar2=None, op0=ALU.mult),
                       reads=['mx', 'identf'], writes=['dg'])
                    pM2 = nextpb(2, 4)
                    grp('pe', [mm(pb[pM2][:, 0:32], onesf[0:32, :], dg[:, :], True, True)], reads=['dg', 'onesf'], writes=['pb%d' % pM2])
                    op('dve', lambda e: e.tensor_copy(out=mxb[:, :], in_=pb[pM2][:, 0:32]), reads=['pb%d' % pM2], writes=['mxb'])
                    op('dve', lambda e: e.tensor_tensor(out=Sall[:, :, :], in0=Sall[:, :, :], in1=mxb[:, :].unsqueeze(1).to_broadcast([128, 17, 32]), op=ALU.subtract),
                       reads=['Sall', 'mxb'], writes=['Sall'])
                    op('act', lambda e: e.activation(out=Pall[:, :, :], in_=Sall[:, :, :], func=AF.Exp, scale=0.125), reads=['Sall'], writes=['Pall'])
                    for s in range(8):
                        h4 = s // 2
                        bank, co = 4 + s // 2, (s % 2) * 256
                        fns = [mm(pb[bank][0:4, co:co + 129], Pall[:, j, s * 4:s * 4 + 4], vs[:, j, h4, 0:129], j == 0, False) for j in range(16)]
                        fns.append(mm(pb[bank][0:4, co:co + 129], Pall[0:4, 16, s * 4:s * 4 + 4], vn[0:4, h4, 0:129], False, True))
                        grp('pe', fns, reads=['Pall', vkey, 'Vn%d' % (b % 2), 'Vn1'] + ['%s_%d' % (vkey, j_) for j_ in range(16)], writes=['pb%d' % bank] if s % 2 == 1 else ['pb%d_lo' % bank])
                    for bank in range(4, 8):
                        evac(osb[:, (bank - 4) * 2:(bank - 4) * 2 + 2, 0:129], pb[bank][0:4, :].rearrange("p (s c) -> p s c", c=256)[:, :, 0:129],
                             reads=['pb%d' % bank, 'pb%d_lo' % bank], writes=['osb%d' % bank], eng='dve' if bank % 2 else 'act')
                    okeys = ['osb%d' % bk for bk in range(4, 8)]
                    op('dve', lambda e: e.reciprocal(out=sfin[:, 0:8], in_=osb[:, :, 128]), reads=okeys, writes=['sfin0'])
                    op('dve', lambda e: e.tensor_tensor(out=osb[:, :, 0:128], in0=osb[:, :, 0:128], in1=sfin[:, 0:8].unsqueeze(2).to_broadcast([4, 8, 128]), op=ALU.mult),
                       reads=okeys + ['sfin0'], writes=['onm'] + okeys)
                    ov = osb[:, :, 0:128].rearrange("p (h c) d -> p h c d", c=2)
                    ats2 = sqj[0:4, :].rearrange("p (h d) -> p h d", d=128)
                    op('dve', lambda e: e.scalar_tensor_tensor(out=ats[:, :, :], in0=ov[:, :, 1, :], scalar=lamw[0:4, 2:3], in1=ov[:, :, 0, :], op0=ALU.mult, op1=ALU.add),
                       reads=['onm', 'neglam'], writes=['ats'])
                    op('dve', lambda e: e.tensor_tensor(out=ats2, in0=ats[:, :, :], in1=ats[:, :, :], op=ALU.mult), reads=['ats'], writes=['sqj'])
                    op('dve', lambda e: e.tensor_reduce(out=sfin[:, 8:12], in_=ats2, axis=AX.X, op=ALU.add), reads=['sqj'], writes=['sfin8'])
                    rstd_from_ss(sfin[:, 8:12], sfin[:, 16:20], sfin[:, 12:16], 128, 'sfin8', 'sfin16')
                    op('dve', lambda e: e.tensor_tensor(out=ats[:, :, :], in0=ats[:, :, :], in1=sfin[:, 16:20].unsqueeze(2).to_broadcast([4, 4, 128]), op=ALU.mult),
                       reads=['ats', 'sfin16'], writes=['ats'])
                    ab = atsb[0]
                    op('dve', lambda e: e.tensor_tensor(out=ab[:, :, :], in0=ats[:, :, :], in1=subw[0:4, :].unsqueeze(1).to_broadcast([4, 4, 128]), op=ALU.mult),
                       reads=['ats', 'subw'], writes=['atsb0'])
                    dma('sp', lambda e: e.dma_start(out=att_scr[b * 4:b * 4 + 4, hh * 512:(hh + 1) * 512], in_=ab[:, :, :].rearrange("p h d -> p (h d)")),
                        key='atsb0', reads=['atsb0'], writes=['att_scr'])
                S.barrier()

            asb = qkb[0]
            dma('sp', lambda e: e.dma_start(out=asb[0:64, :], in_=att_scr[:, 0:512]), key='asb', writes=['asb'])
            asb2 = qkb[1]
            dma('sp', lambda e: e.dma_start(out=asb2[0:64, :], in_=att_scr[:, 512:1024]), key='asb2', writes=['asb2'])
            for hh in range(2):
                src = asb if hh == 0 else asb2
                pj = nextpb(0, 4)
                grp('pe', [tp(pbh[pj][:, h4 * 128:h4 * 128 + 64], src[0:64, h4 * 128:(h4 + 1) * 128], ident[:64, :64]) for h4 in range(4)],
                    reads=['asb', 'asb2', 'ident'], writes=['pb%d' % pj])
                evac(attT[:, hh * 4:hh * 4 + 4, ocol(SU):ocol(SU) + 64], pbh[pj][:, 0:512].rearrange("p (j t) -> p j t", j=4)[:, :, :64],
                     reads=['pb%d' % pj], writes=['attTs_%d' % hh])
            S.barrier()

        if KSTAGE >= 1:
            p2.close()
        yT_scr = dscr("yT_scr", [128, 8 * NOWNCOL], BF16)
        yT_scr3 = yT_scr.rearrange("p (a b) -> p a b", a=8)
        with ExitStack() as p3:
            xbcT = sb(p3, "xbcT", [128, 12, NCOL], BF16)
            z_scr = dscr("z_scr", [10, 128, 1024], BF16)
            zl = sb(p3, "zl", [128, 1024], BF16)
            dtr = sb(p3, "dtr", [128, NU, 16], F32)
            dtt = sb(p3, "dtt", [128, NU, 16], F32)
            aa = sb(p3, "aa", [128, NU, 16], F32)
            sm = sb(p3, "sm", [128, 8, 16], F32)
            csw = sb(p3, "csw", [128, 12, 4], F32)
            csb = sb(p3, "csb", [128, 12], F32)
            cso = sb(p3, "cso", [128, 12, 3], F32)
            dma('sp', lambda e: e.dma_start(out=sm[:, 0, :], in_=dtbd), key='c0', writes=['sm0'])
            dma('sp', lambda e: e.dma_start(out=sm[:, 1, :], in_=alogd), key='c1', writes=['sm1'])
            dma('sp', lambda e: e.dma_start(out=sm[:, 2, :], in_=dskd), key='c2', writes=['sm2'])
            dma('sp', lambda e: e.dma_start(out=csw[:], in_=cswd), key='c3', writes=['csw'])
            dma('sp', lambda e: e.dma_start(out=csb[:], in_=csbd), key='c4', writes=['csb'])
            op('act', lambda e: e.activation(out=sm[:, 1, :], in_=sm[:, 1, :], func=AF.Exp), reads=['sm1'], writes=['sm1'])
            op('dve', lambda e: e.tensor_scalar(out=sm[:, 1, :], in0=sm[:, 1, :], scalar1=-1.0, scalar2=None, op0=ALU.mult), reads=['sm1'], writes=['sm1'])
            with ExitStack() as p3a:
                wb3 = sb(p3a, "wb3", [128, 16, 512], BF16)
                xps = sb(p3a, "xps", [128, 12, 16, 7], F32)
                zt = [sb(p3a, "zt%d" % i, [128, 512], BF16) for i in range(2)]
                xpre = sb(p3a, "xpre", [128, 3 + 2048], F32)
                acc = sb(p3a, "acc", [128, 2048], F32)
                stc = [sb(p3a, "stc%d" % i, [48, 128], F32) for i in range(2)]
                xst = sb(p3a, "xst", [64, 512], F32)
                op('pool', lambda e: e.memset(xpre[:, 0:3], 0.0), writes=['xpre0'])
                for j in range(12):
                    pi = nextpb(0, 4)
                    dma('sp', lambda e: e.dma_start(out=stc[j % 2][:, :], in_=st_cssd[:, j * 128:(j + 1) * 128]), key='stc%d' % (j % 2), writes=['stc%d' % (j % 2)])
                    grp('pe', [tp(pb[pi][:, 0:48], stc[j % 2][0:48, :], identf[0:48, 0:48])], reads=['stc%d' % (j % 2), 'identf'], writes=['pb%d' % pi])
                    evac(xps[:, j, :, 0:3], pb[pi][:, 0:48].rearrange("p (b t) -> p b t", t=3), reads=['pb%d' % pi], writes=['xps%d' % j])
                groups = [(0, 512), (512, 512), (1024, 512), (1536, 512)]
                for cg in range(3):
                    dma('pool', lambda e: e.dma_start(out=wb3[:, :, :], in_=w_in[:, 4096 + cg * 512:4096 + (cg + 1) * 512].rearrange("(kc p) n -> p kc n", p=128)),
                        key='wb3', writes=['wb3'])
                    pi = nextpb(0, 4)
                    grp('pe', [mm(pb[pi][:64, :], hT[:, kc, 2048:2112], wb3[:, kc, :], kc == 0, kc == 15) for kc in range(16)],
                        reads=hT_keys(SU) + ['wb3'], writes=['pb%d' % pi])
                    evac(xst[:, :], pb[pi][:64, :], reads=['pb%d' % pi], writes=['xst'], eng='act')
                    for t in range(1, 4):
                        dma('sp', lambda e: e.dma_start(out=CSSDS[:, t - 1, cg * 512:(cg + 1) * 512], in_=xst[t:64:4, :]), key='xst', reads=['xst'])
                    for jj in range(4):
                        j = cg * 4 + jj
                        for (g0, ng) in groups:
                            pi = nextpb(0, 4)
                            grp('pe', [mm(pb[pi][:, :ng], wb3[:, kc, jj * 128:(jj + 1) * 128], hT[:, kc, g0:g0 + ng], kc == 0, kc == 15) for kc in range(16)],
                                reads=['wb3'] + [k_ for u_ in range(g0 // 128, (g0 + ng) // 128) for k_ in hT_keys(u_)], writes=['pb%d' % pi])
                            evac(xpre[:, 3 + g0:3 + g0 + ng], pb[pi][:, :ng], reads=['pb%d' % pi], writes=['xpre%d' % g0])
                        pi = nextpb(0, 4)
                        grp('pe', [mm(pb[pi][:, :64], wb3[:, kc, jj * 128:(jj + 1) * 128], hT[:, kc, 2048:2112], kc == 0, kc == 15) for kc in range(16)],
                            reads=['wb3'] + hT_keys(SU), writes=['pb%d' % pi])
                        evac(xps[:, j, :, 3:7], pb[pi][:, 0:64].rearrange("p (b t) -> p b t", t=4), reads=['pb%d' % pi], writes=['xpsn%d' % j])
                        xk = ['xpre0'] + ['xpre%d' % g for g, _ in groups]
                        op('act', lambda e: e.copy(out=cso[:, j, :], in_=xpre[:, 2048:2051]), reads=xk, writes=['cso'])
                        op('dve', lambda e: e.tensor_scalar(out=acc[:, :], in0=xpre[:, 3:2051], scalar1=csw[:, j, 3:4], scalar2=csb[:, j:j + 1], op0=ALU.mult, op1=ALU.add),
                           reads=xk + ['csw', 'csb'], writes=['acc'])
                        for i in (2, 1, 0):
                            op('dve', lambda e: e.scalar_tensor_tensor(out=acc[:, :], in0=xpre[:, i:i + 2048], scalar=csw[:, j, i:i + 1], in1=acc[:, :], op0=ALU.mult, op1=ALU.add),
                               reads=xk + ['csw', 'acc'], writes=['acc'])
                        op('act', lambda e: e.activation(out=xbcT[:, j, 0:2048], in_=acc[:, :], func=AF.Silu), reads=['acc'], writes=['xbcT%d' % j])
                        a4 = acc[:, 0:64].rearrange("p (b t) -> p b t", t=4)
                        op('dve', lambda e: e.tensor_scalar(out=a4, in0=xps[:, j, :, 3:7], scalar1=csw[:, j, 3:4], scalar2=csb[:, j:j + 1], op0=ALU.mult, op1=ALU.add),
                           reads=['xps%d' % j, 'xpsn%d' % j, 'csw', 'csb', 'acc'], writes=['acc'])
                        for i in (2, 1, 0):
                            op('dve', lambda e: e.scalar_tensor_tensor(out=a4, in0=xps[:, j, :, i:i + 4], scalar=csw[:, j, i:i + 1], in1=a4, op0=ALU.mult, op1=ALU.add),
                               reads=['xps%d' % j, 'xpsn%d' % j, 'csw', 'acc'], writes=['acc'])
                        op('act', lambda e: e.activation(out=xbcT[:, j, 2048:2112], in_=acc[:, 0:64], func=AF.Silu), reads=['acc'], writes=['xbcTs%d' % j])
                for t in range(3):
                    dma('sp', lambda e: e.dma_start(out=CSSDP[t].rearrange("(j f) -> f j", f=128), in_=cso[:, :, t], allow_slow_non_contiguous=True), key='cso', reads=['cso'])
                for cg in range(2):
                    dma('pool', lambda e: e.dma_start(out=wb3[:, :, :], in_=w_in[:, 3072 + cg * 512:3072 + (cg + 1) * 512].rearrange("(kc p) n -> p kc n", p=128)),
                        key='wb3', writes=['wb3'])
                    for u in range(OWN0, NU):
                        n = nu(u)
                        pi = nextpb(0, 4)
                        grp('pe', [mm(pb[pi][:n, :], hT[:, kc, u * 128:u * 128 + n], wb3[:, kc, :], kc == 0, kc == 15) for kc in range(16)],
                            reads=hT_keys(u) + ['wb3'], writes=['pb%d' % pi])
                        evac(zt[u % 2][:n, :], pb[pi][:n, :], reads=['pb%d' % pi], writes=['zt%d' % (u % 2)])
                        dma('sp', lambda e: e.dma_start(out=z_scr[u - OWN0, :n, cg * 512:(cg + 1) * 512], in_=zt[u % 2][:n, :]), key='zt%d' % (u % 2), reads=['zt%d' % (u % 2)], writes=['z_scr'])
                dma('pool', lambda e: e.dma_start(out=wb3[:, :, 0:16], in_=w_in[:, 5632:5648].rearrange("(kc p) n -> p kc n", p=128)), key='wb3', writes=['wb3'])
                op('pool', lambda e: e.memset(dtr[:], 0.0), writes=['dtr'])
                for u in range(NU):
                    n = nu(u)
                    pi = nextpb(0, 4)
                    grp('pe', [mm(pb[pi][:n, 0:16], hT[:, kc, u * 128:u * 128 + n], wb3[:, kc, 0:16], kc == 0, kc == 15) for kc in range(16)],
                        reads=hT_keys(u) + ['wb3'], writes=['pb%d' % pi])
                    evac(dtr[:n, u, :], pb[pi][:n, 0:16], reads=['pb%d' % pi, 'dtr'], writes=['dtr%d' % u])
                S.barrier()
            bc17 = lambda ap: ap.unsqueeze(1).to_broadcast([128, NU, 16])
            op('dve', lambda e: e.tensor_tensor(out=dtr[:], in0=dtr[:], in1=bc17(sm[:, 0, :]), op=ALU.add), reads=['sm0'], writes=['dtr'])
            op('act', lambda e: e.activation(out=dtt[:], in_=dtr[:], func=AF.Abs), reads=['dtr'], writes=['dtt'])
            op('act', lambda e: e.activation(out=dtt[:], in_=dtt[:], func=AF.Exp, scale=-1.0), reads=['dtt'], writes=['dtt'])
            op('act', lambda e: e.activation(out=dtt[:], in_=dtt[:], func=AF.Ln, bias=onesf[:, 0:1]), reads=['dtt', 'onesf'], writes=['dtt'])
            op('dve', lambda e: e.scalar_tensor_tensor(out=dtt[:], in0=dtr[:], scalar=0.0, in1=dtt[:], op0=ALU.max, op1=ALU.add), reads=['dtr', 'dtt'], writes=['dtt'])
            op('dve', lambda e: e.tensor_tensor(out=aa[:], in0=dtt[:], in1=bc17(sm[:, 1, :]), op=ALU.mult), reads=['dtt', 'sm1'], writes=['aa'])
            S.barrier()

            t1 = sb(p3, "t1", [128, 1024], F32)
            ytk = sb(p3, "ytk", [128, 1024], F32)
            zs = sb(p3, "zs", [128, 1024], F32)
            ynb = sb(p3, "ynb", [128, 1024], BF16)
            nsw_t = sb(p3, "nsw_t", [128, 1024], F32)
            yTt = [sb(p3, "yTt%d" % i, [128, 8, 128], BF16) for i in range(1)] * 2
            gst = sb(p3, "gst", [128, 8], F32)
            p3p = ExitStack()
            hst = sb(p3p, "hst", [128, 1024], F32)
            hbf = [sb(p3p, "hbf%d" % i, [128, 1024], BF16) for i in range(2)]
            xdt = sb(p3p, "xdt", [128, 1024], BF16)
            xsd = sb(p3p, "xsd", [128, 1024], BF16)
            xdd = sb(p3p, "xdd", [128, 1024], BF16)
            btk = sb(p3p, "btk", [128, 256], BF16)
            cst = sb(p3p, "cst", [128, 8, 16], F32)
            abc = sb(p3p, "abc", [128, 16, 128], F32)
            cbS = sb(p3p, "cbS", [128, 2, 128], F32)
            LT = [sb(p3p, "LT%d" % i, [128, 128], F32) for i in range(2)]
            MT = [sb(p3p, "MT%d" % i, [128, 128], BF16) for i in range(2)]
            sso = sb(p3p, "sso", [128, 8, 128], F32)
            dma('sp', lambda e: e.dma_start(out=nsw_t[:], in_=nsw), key='c6', writes=['nsw'])
            op('pool', lambda e: e.memset(hst[:], 0.0), writes=['hst'])
            op('pool', lambda e: e.memset(hbf[0][:], 0.0), writes=['hbf0'])
            bc64 = lambda ap, P=128, H=16: ap.unsqueeze(2).to_broadcast([P, H, 64])
            v3 = lambda ap: ap.rearrange("p (h d) -> p h d", d=64)

            def gate_norm_store(u, n, ysrc_key):
                dma('sp', lambda e: e.dma_start(out=zl[:n, :], in_=z_scr[u - OWN0, :n, :]), key='zl', reads=['z_scr'], writes=['zl'])
                op('act', lambda e: e.activation(out=zs[:n, :], in_=zl[:n, :], func=AF.Silu), reads=['zl'], writes=['zs'])
                op('dve', lambda e: e.tensor_tensor(out=ytk[:n, :], in0=ytk[:n, :], in1=zs[:n, :], op=ALU.mult), reads=[ysrc_key, 'zs'], writes=['ytk'])
                op('dve', lambda e: e.memset(gst[:, 0:2], 0.0), writes=['gst0'])
                for g in range(2):
                    op('act', lambda e: e.activation(out=junk[:n, 0:512], in_=ytk[:n, g * 512:(g + 1) * 512], func=AF.Square, accum_out=gst[:n, g:g + 1]),
                       reads=['ytk', 'gst0'], writes=['junk', 'gsta%d' % g, 'gsta'])
                rstd_from_ss(gst[:n, 0:2], gst[:n, 4:6], gst[:n, 2:4], 512, 'gsta', 'gst4')
                for g in range(2):
                    op('dve', lambda e: e.scalar_tensor_tensor(out=ynb[:n, g * 512:(g + 1) * 512], in0=ytk[:n, g * 512:(g + 1) * 512], scalar=gst[:n, 4 + g:5 + g],
                                                              in1=nsw_t[:n, g * 512:(g + 1) * 512], op0=ALU.mult, op1=ALU.mult),
                       reads=['ytk', 'gst4', 'gsta1', 'nsw'], writes=['ynb%d' % g])
                pj = nextpb(0, 4)
                grp('pe', [tp(pbh[pj][:, j * 128:j * 128 + n], ynb[:n, j * 128:(j + 1) * 128], ident[:n, :n]) for j in range(8)],
                    reads=['ynb0', 'ynb1', 'ident'], writes=['pb%d' % pj])
                yt_ = yTt[u % 2]
                evac(yt_[:, :, :n], pbh[pj][:, :].rearrange("p (j t) -> p j t", j=8)[:, :, :n], reads=['pb%d' % pj], writes=['yTt0'])
                dma('sp', lambda e: e.dma_start(out=yT_scr3[:, :, ocol(u):ocol(u) + n], in_=yt_[:, :, :n]), key='yTt0', reads=['yTt0'], writes=['yT_scr'])

            for c in range(16 if KSTAGE >= 4 else 0):
                cc = slice(c * 128, (c + 1) * 128)
                hprev, hnext = hbf[c % 2], hbf[(c + 1) % 2]
                a_c = aa[:, c, :]
                pi = nextpb(0, 4)
                grp('pe', [mm(pb[pi][:, 0:16], triu[:, :], a_c, True, True), mm(pb[pi][:, 16:32], onesf[:, :], a_c, True, True)],
                    reads=['aa', 'triu', 'onesf'], writes=['pb%d' % pi])
                op('dve', lambda e: e.tensor_copy(out=cst[:, 0:2, :], in_=pb[pi][:, 0:32].rearrange("p (a h) -> p a h", a=2)), reads=['pb%d' % pi], writes=['cst01'])
                op('act', lambda e: e.activation(out=cst[:, 2:4, :], in_=cst[:, 0:2, :], func=AF.Exp), reads=['cst01'], writes=['cst23'])
                op('dve', lambda e: e.tensor_tensor(out=cst[:, 4, :], in0=cst[:, 1, :], in1=cst[:, 0, :], op=ALU.subtract), reads=['cst01'], writes=['cst4'])
                op('act', lambda e: e.activation(out=cst[:, 4, :], in_=cst[:, 4, :], func=AF.Exp), reads=['cst4'], writes=['cst4'])
                op('dve', lambda e: e.tensor_scalar(out=cst[:, 5, :], in0=cst[:, 0, :], scalar1=-1.0, scalar2=None, op0=ALU.mult), reads=['cst01'], writes=['cst5'])
                pa, pbk = nextpb(0, 4), nextpb(0, 4)
                grp('pe', [tp(pbh[pa][:, j * 128:(j + 1) * 128], xbcT[:, j, cc], ident[:, :]) for j in range(8)],
                    reads=['xbcT%d' % j for j in range(8)] + ['ident'], writes=['pb%d' % pa])
                grp('pe', [tp(pbh[pbk][:, j * 128:(j + 1) * 128], xbcT[:, 8 + j, cc], ident[:, :]) for j in range(2)],
                    reads=['xbcT8', 'xbcT9', 'ident'], writes=['pb%d' % pbk])
                op('dve', lambda e: e.tensor_tensor(out=v3(xdt[:, :]), in0=v3(pbh[pa][:, :]), in1=bc64(dtt[:, c, :]), op=ALU.mult), reads=['pb%d' % pa, 'dtt'], writes=['xdt'])
                if c >= OWN0:
                    op('dve', lambda e: e.tensor_tensor(out=v3(xsd[:, :]), in0=v3(pbh[pa][:, :]), in1=bc64(sm[:, 2, :]), op=ALU.mult), reads=['pb%d' % pa, 'sm2'], writes=['xsd'])
                op('act', lambda e: e.copy(out=btk[:, :], in_=pbh[pbk][:, 0:256]), reads=['pb%d' % pbk], writes=['btk'])
                op('dve', lambda e: e.tensor_tensor(out=v3(xdd[:, :]), in0=v3(xdt[:, :]), in1=bc64(cst[:, 4, :]), op=ALU.mult), reads=['xdt', 'cst4'], writes=['xdd'])
                if c >= OWN0:
                    op('dve', lambda e: e.tensor_copy(out=abc[:, :, :], in_=a_c.unsqueeze(2).to_broadcast([128, 16, 128])), reads=['aa'], writes=['abc'])
                    for g in range(2):
                        pi = nextpb(0, 4)
                        grp('pe', [mm(pb[pi][:, 0:128], xbcT[:, 8 + g, cc], xbcT[:, 10 + g, cc], True, True)], reads=['xbcT%d' % (8 + g), 'xbcT%d' % (10 + g)], writes=['pb%d' % pi])
                        evac(cbS[:, g, :], pb[pi][:, 0:128], reads=['pb%d' % pi], writes=['cbS%d' % g])
                    for g in range(2):
                        pY = 4 + g
                        for r in range(8):
                            h = g * 8 + r
                            pi = nextpb(0, 4)
                            grp('pe', [mm(pb[pi][:, 0:128], abc[:, h, :], triu[:, :], True, False), mm(pb[pi][:, 0:128], identf[:, :], nmaskf[:, :], False, True)],
                                reads=['abc', 'triu', 'identf', 'nmaskf'], writes=['pb%d' % pi])
                            lt, mt = LT[h % 2], MT[h % 2]
                            op('act', lambda e: e.activation(out=lt[:, :], in_=pb[pi][:, 0:128], func=AF.Exp, bias=cst[:, 5, h:h + 1]),
                               reads=['pb%d' % pi, 'cst5'], writes=['LT%d' % (h % 2)])
                            op('dve', lambda e: e.tensor_tensor(out=mt[:, :], in0=lt[:, :], in1=cbS[:, g, :], op=ALU.mult), reads=['LT%d' % (h % 2), 'cbS%d' % g], writes=['MT%d' % (h % 2)])
                            grp('pe', [mm(pb[pY][:, r * 64:(r + 1) * 64], mt[:, :], xdt[:, h * 64:(h + 1) * 64], True, True)],
                                reads=['MT%d' % (h % 2), 'xdt'], writes=['pb%d' % pY] if r == 7 else ['pb%d_p' % pY])
                        pO = 6 + g
                        grp('pe', [mm(pb[pO][:, :], xbcT[:, 10 + g, cc], hprev[:, g * 512:(g + 1) * 512], True, True)],
                            reads=['xbcT%d' % (10 + g), 'hbf%d' % (c % 2)], writes=['pb%d' % pO])
                        gs = slice(g * 512, (g + 1) * 512)
                        op('dve', lambda e: e.tensor_tensor(out=v3(t1[:, gs]), in0=v3(pb[pO][:, :]), in1=bc64(cst[:, 2, g * 8:(g + 1) * 8], 128, 8), op=ALU.mult),
                           reads=['pb%d' % pO, 'cst23'], writes=['t1_%d' % g])
                        op('dve', lambda e: e.tensor_tensor(out=t1[:, gs], in0=pb[pY][:, :], in1=t1[:, gs], op=ALU.add), reads=['pb%d' % pY, 'pb%d_p' % pY, 't1_%d' % g], writes=['t1_%d' % g])
                        op('dve', lambda e: e.tensor_tensor(out=ytk[:, gs], in0=t1[:, gs], in1=xsd[:, gs], op=ALU.add), reads=['t1_%d' % g, 'xsd'], writes=['ytk'])
                    gate_norm_store(c, 128, 'ytk')
                for g in range(2):
                    pi = nextpb(0, 4)
                    gs = slice(g * 512, (g + 1) * 512)
                    grp('pe', [mm(pb[pi][:, :], btk[:, g * 128:(g + 1) * 128], xdd[:, gs], True, True)], reads=['btk', 'xdd'], writes=['pb%d' % pi])
                    op('dve', lambda e: e.tensor_tensor(out=v3(hst[:, gs]), in0=v3(hst[:, gs]), in1=bc64(cst[:, 3, g * 8:(g + 1) * 8], 128, 8), op=ALU.mult),
                       reads=['hst', 'cst23'], writes=['hst'])
                    op('dve', lambda e: e.tensor_tensor(out=hst[:, gs], in0=pb[pi][:, :], in1=hst[:, gs], op=ALU.add), reads=['pb%d' % pi, 'hst'], writes=['hst'])
                if c == OWN0:
                    op('dve', lambda e: e.tensor_scalar(out=hst[:, :], in0=hst[:, :], scalar1=valid[:, 0:1], scalar2=None, op0=ALU.mult), reads=['hst', 'valid'], writes=['hst'])
                op('act', lambda e: e.copy(out=hnext[:, :], in_=hst[:, :]), reads=['hst'], writes=['hbf%d' % ((c + 1) % 2)])
            for half in range(2):
                pi = nextpb(0, 4)
                grp('pe', [tp(pb[pi][:, j * 128:(j + 1) * 128], hst[:, (half * 4 + j) * 128:(half * 4 + j + 1) * 128], identf[:, :]) for j in range(4)],
                    reads=['hst', 'identf'], writes=['pb%d' % pi])
                evac(sso[:, half * 4:half * 4 + 4, :], pb[pi][:, :].rearrange("p (j n) -> p j n", j=4), reads=['pb%d' % pi], writes=['sso%d' % half])
            dma('sp', lambda e: e.dma_start(out=SSMP.rearrange("(j h2) p n -> (h2 p) j n", h2=2), in_=sso[:, :, :]), key='sso', reads=['sso0', 'sso1'])
            S.barrier()
            p3p.close()

            with ExitStack() as p3s:
                xs64 = sb(p3s, "xs64", [64, 1280], BF16)
                dts = sb(p3s, "dts", [4, 16, 16], F32)
                a_s = sb(p3s, "a_s", [4, 16, 16], F32)
                cs = sb(p3s, "cs", [4, 6, 256], F32)
                totr = sb(p3s, "totr", [128, 256], F32)
                etc = sb(p3s, "etc", [128, 16, 8], F32)
                cdg = sb(p3s, "cdg", [4, 256, 4], F32)
                LTs = sb(p3s, "LTs", [4, 256, 4], F32)
                MTs = sb(p3s, "MTs", [4, 256, 4], BF16)
                cbs = sb(p3s, "cbs", [4, 32, 4], F32)
                h0 = [sb(p3s, "h0_%d" % i, [128, 8, 128], F32) for i in range(1)] * 2
                h0T = sb(p3s, "h0T", [128, 1024], BF16)
                hn = h0
                xb4 = [sb(p3s, "xb4_%d" % i, [4, 1280], BF16) for i in range(1)] * 2
                xdt4 = sb(p3s, "xdt4", [4, 1024], BF16)
                xsd4 = sb(p3s, "xsd4", [4, 1024], BF16)
                xdd4 = sb(p3s, "xdd4", [4, 1024], BF16)
                y4 = [ytk[0:4, :]] * 2
                for half in range(2):
                    pj = nextpb(0, 4)
                    js = list(range(half * 5, half * 5 + 5))
                    grp('pe', [tp(pbh[pj][0:64, k * 128:(k + 1) * 128], xbcT[:, j, 2048:2112], ident[:, :]) for k, j in enumerate(js)],
                        reads=['xbcTs%d' % j for j in js] + ['ident'], writes=['pb%d' % pj])
                    evac(xs64[:, half * 640:(half + 1) * 640], pbh[pj][0:64, 0:640], reads=['pb%d' % pj], writes=['xs64_%d' % half])
                dma('sp', lambda e: e.dma_start(out=xbcs_scr, in_=xs64[:, :]), key='xs64', reads=['xs64_0', 'xs64_1'], writes=['xbcs_scr'])
                dma('sp', lambda e: e.dma_start(out=dts_scr, in_=dtt[0:64, SU, :]), key='dts', reads=['dtt'], writes=['dts_scr'])
                dma('sp', lambda e: e.dma_start(out=dts[:, :, :], in_=dts_scr.rearrange("(b t) h -> t b h", t=4)), key='dts2', reads=['dts_scr'], writes=['dts'])
                f2 = lambda ap: ap.rearrange("p b h -> p (b h)")
                op('dve', lambda e: e.tensor_tensor(out=a_s[:, :, :], in0=dts[:, :, :], in1=sm[0:4, 1, :].unsqueeze(1).to_broadcast([4, 16, 16]), op=ALU.mult), reads=['dts', 'sm1'], writes=['a_s'])
                pi = nextpb(0, 4)
                pr = nextpb(0, 4)
                grp('pe', [mm(pb[pi][0:4, 0:256], triu[0:4, 0:4], f2(a_s[:, :, :]), True, True), mm(pb[pr][:, 0:256], onesf[0:4, :], f2(a_s[:, :, :]), True, True)],
                    reads=['a_s', 'triu', 'onesf'], writes=['pb%d' % pi, 'pb%d' % pr])
                op('dve', lambda e: e.tensor_copy(out=cs[:, 0, :], in_=pb[pi][0:4, 0:256]), reads=['pb%d' % pi], writes=['cs0'])
                op('dve', lambda e: e.tensor_copy(out=totr[:, :], in_=pb[pr][:, 0:256]), reads=['pb%d' % pr], writes=['totr'])
                op('act', lambda e: e.activation(out=cs[:, 1, :], in_=cs[:, 0, :], func=AF.Exp), reads=['cs0'], writes=['cs1'])
                op('dve', lambda e: e.tensor_tensor(out=cs[:, 2, :], in0=totr[0:4, :], in1=cs[:, 0, :], op=ALU.subtract), reads=['totr', 'cs0'], writes=['cs2'])
                op('act', lambda e: e.activation(out=cs[:, 2, :], in_=cs[:, 2, :], func=AF.Exp), reads=['cs2'], writes=['cs2'])
                tv = totr[:, :].rearrange("p (b hp h2) -> p b hp h2", hp=8, h2=2)
                op('act', lambda e: e.activation(out=etc[0:64, :, :], in_=tv[0:64, :, :, 0], func=AF.Exp), reads=['totr'], writes=['etc0'])
                op('act', lambda e: e.activation(out=etc[64:128, :, :], in_=tv[64:128, :, :, 1], func=AF.Exp), reads=['totr'], writes=['etc1'])
                op('dve', lambda e: e.tensor_tensor(out=cdg[:, :, :], in0=cs[:, 0, :].unsqueeze(2).to_broadcast([4, 256, 4]),
                                                    in1=identf[0:4, 0:4].unsqueeze(1).to_broadcast([4, 256, 4]), op=ALU.mult), reads=['cs0', 'identf'], writes=['cdg'])
                p0, p1 = 4, 5
                cf = cdg[:, :, :].rearrange("p a t -> p (a t)")
                grp('pe', [mm(pb[p0][0:4, :], onesf[0:4, 0:4], cf[:, 0:512], True, True), mm(pb[p1][0:4, :], onesf[0:4, 0:4], cf[:, 512:1024], True, True)],
                    reads=['cdg', 'onesf'], writes=['pb4', 'pb5'])
                for hf, pp in ((0, p0), (1, p1)):
                    lv = LTs[:, hf * 128:(hf + 1) * 128, :]
                    op('dve', lambda e: e.tensor_tensor(out=lv, in0=pb[pp][0:4, :].rearrange("p (a t) -> p a t", t=4),
                                                        in1=cs[:, 0, hf * 128:(hf + 1) * 128].unsqueeze(2).to_broadcast([4, 128, 4]), op=ALU.subtract),
                       reads=['pb%d' % pp, 'cs0'], writes=['LTs%d' % hf])
                    op('dve', lambda e: e.tensor_tensor(out=lv, in0=lv, in1=mk4s[:, :].unsqueeze(1).to_broadcast([4, 128, 4]), op=ALU.add), reads=['LTs%d' % hf, 'mk4s'], writes=['LTs%d' % hf])
                    op('act', lambda e: e.activation(out=lv, in_=lv, func=AF.Exp), reads=['LTs%d' % hf], writes=['LTs%d' % hf])
                pi = nextpb(0, 4)
                fns = []
                for b in range(16):
                    for g in range(2):
                        fns.append(mm(pb[pi][0:4, (b * 2 + g) * 4:(b * 2 + g) * 4 + 4], xbcT[:, 8 + g, 2048 + b * 4:2048 + b * 4 + 4], xbcT[:, 10 + g, 2048 + b * 4:2048 + b * 4 + 4], True, True))
                grp('pe', fns, reads=['xbcTs8', 'xbcTs9', 'xbcTs10', 'xbcTs11'], writes=['pb%d' % pi])
                op('dve', lambda e: e.tensor_copy(out=cbs[:, :, :], in_=pb[pi][0:4, 0:128].rearrange("p (a t) -> p a t", t=4)), reads=['pb%d' % pi], writes=['cbs'])
                op('dve', lambda e: e.tensor_tensor(out=MTs[:, :, :].rearrange("p (a r) t -> p a r t", r=8), in0=LTs[:, :, :].rearrange("p (a r) t -> p a r t", r=8),
                                                    in1=cbs[:, :, :].unsqueeze(2).to_broadcast([4, 32, 8, 4]), op=ALU.mult), reads=['LTs0', 'LTs1', 'cbs'], writes=['MTs'])
                bc4 = lambda ap, H=16: ap.unsqueeze(2).to_broadcast([4, H, 64])
                for b in range(16 if KSTAGE >= 5 else 0):
                    k = 0
                    xb_, h0_, hn_, y4_ = xb4[k], h0[k], hn[k], y4[k]
                    dma('sp', lambda e: e.dma_start(out=xb_[:, :], in_=xbcs_scr[b * 4:b * 4 + 4, :]), key='xb4_%d' % k, reads=['xbcs_scr'], writes=['xb4_%d' % k])
                    dma('sp', lambda e: e.dma_start(out=h0_[:, :, :], in_=st_ssm[b].rearrange("(hp h2) p n -> (h2 p) hp n", h2=2)), key='h0_%d' % k, writes=['h0_%d' % k])
                    for half in range(2):
                        pi = 4 + half
                        grp('pe', [tp(pb[pi][:, j * 128:(j + 1) * 128], h0_[:, half * 4 + j, :], identf[:, :]) for j in range(4)], reads=['h0_%d' % k, 'identf'], writes=['pb%d' % pi])
                        evac(h0T[:, half * 512:(half + 1) * 512], pb[pi][:, :], reads=['pb%d' % pi], writes=['h0T%d' % half])
                    xs_ = xb_[:, 0:1024]
                    op('dve', lambda e: e.tensor_tensor(out=v3(xdt4[:, :]), in0=v3(xs_), in1=bc4(dts[:, b, :]), op=ALU.mult), reads=['xb4_%d' % k, 'dts'], writes=['xdt4'])
                    op('dve', lambda e: e.tensor_tensor(out=v3(xsd4[:, :]), in0=v3(xs_), in1=bc4(sm[0:4, 2, :]), op=ALU.mult), reads=['xb4_%d' % k, 'sm2'], writes=['xsd4'])
                    op('dve', lambda e: e.tensor_tensor(out=v3(xdd4[:, :]), in0=v3(xdt4[:, :]), in1=bc4(cs[:, 2, b * 16:(b + 1) * 16]), op=ALU.mult), reads=['xdt4', 'cs2'], writes=['xdd4'])
                    pY0, pY1 = 0, 1
                    fns = [mm(pb[pY0 + h // 8][0:4, (h % 8) * 64:(h % 8 + 1) * 64], MTs[0:4, b * 16 + h, :], xdt4[0:4, h * 64:(h + 1) * 64], True, True) for h in range(16)]
                    grp('pe', fns, reads=['MTs', 'xdt4'], writes=['pb0', 'pb1'])
                    grp('pe', [mm(pb[2 + g][0:4, :], xbcT[:, 10 + g, 2048 + b * 4:2048 + b * 4 + 4], h0T[:, g * 512:(g + 1) * 512], True, True) for g in range(2)],
                        reads=['xbcTs10', 'xbcTs11', 'h0T0', 'h0T1'], writes=['pb2', 'pb3'])
                    for g in range(2):
                        gs = slice(g * 512, (g + 1) * 512)
                        op('dve', lambda e: e.tensor_tensor(out=v3(y4_[:, gs]), in0=v3(pb[2 + g][0:4, :]), in1=bc4(cs[:, 1, b * 16 + g * 8:b * 16 + g * 8 + 8], 8), op=ALU.mult),
                           reads=['pb%d' % (2 + g), 'cs1'], writes=['y4_%d_%d' % (k, g)])
                        op('dve', lambda e: e.tensor_tensor(out=y4_[:, gs], in0=pb[g][0:4, :], in1=y4_[:, gs], op=ALU.add), reads=['pb%d' % g, 'y4_%d_%d' % (k, g)], writes=['y4_%d_%d' % (k, g)])
                        op('dve', lambda e: e.tensor_tensor(out=y4_[:, gs], in0=y4_[:, gs], in1=xsd4[:, gs], op=ALU.add), reads=['y4_%d_%d' % (k, g), 'xsd4'], writes=['y4_%d_%d' % (k, g)])
                    dma('sp', lambda e: e.dma_start(out=ys_scr[b * 4:b * 4 + 4, :], in_=y4_[:, :]), key='y4_%d' % k, reads=['y4_%d_0' % k, 'y4_%d_1' % k], writes=['ys_scr'])
                    for half in range(2):
                        pi = 6 + half
                        grp('pe', [mm(pb[pi][:, j * 128:(j + 1) * 128], xdd4[0:4, (half * 4 + j) * 128:(half * 4 + j + 1) * 128],
                                      xb_[0:4, 1024 + half * 128:1024 + (half + 1) * 128], True, True) for j in range(4)],
                            reads=['xdd4', 'xb4_%d' % k], writes=['pb%d' % pi])
                        for j in range(4):
                            hp = half * 4 + j
                            op('dve', lambda e: e.scalar_tensor_tensor(out=hn_[:, hp, :], in0=h0_[:, hp, :], scalar=etc[:, b, hp:hp + 1], in1=pb[pi][:, j * 128:(j + 1) * 128],
                                                                      op0=ALU.mult, op1=ALU.add),
                               reads=['h0_%d' % k, 'etc0', 'etc1', 'pb%d' % pi], writes=['h0_%d' % k])
                    dma('sp', lambda e: e.dma_start(out=SSMS[b].rearrange("(hp h2) p n -> (h2 p) hp n", h2=2), in_=hn_[:, :, :]), key='hn%d' % k,
                        reads=['h0_%d' % k])
                S.barrier()
                dma('sp', lambda e: e.dma_start(out=ytk[0:64, :], in_=ys_scr), key='ytk', writes=['ytk'])
                gate_norm_store(SU, 64, 'ytk')
                S.barrier()
        p12.close()

        with ExitStack() as p4:
            xmid = sb(p4, "xmid", [128, 10, D], F32)
            with ExitStack() as p4a:
                yT = sb(p4a, "yT", [128, 8, NOWNCOL], BF16)
                wb4 = sb(p4a, "wb4", [128, 16, 512], BF16)
                xr = [sb(p4a, "xr%d" % i, [128, 512], F32) for i in range(2)]
                dma('sp', lambda e: e.dma_start(out=yT[:, :, :], in_=yT_scr3), key='yT', writes=['yT'])
                xi = 0
                for cg in range(4):
                    dma('pool', lambda e: e.dma_start(out=wb4[:, :, :], in_=w_out[:, cg * 512:(cg + 1) * 512].rearrange("(kc p) n -> p kc n", p=128)), key='wb4', writes=['wb4'])
                    for u in range(OWN0, NU):
                        n = nu(u)
                        x_ = xr[xi % 2]
                        xk = 'xr%d' % (xi % 2)
                        xi += 1
                        dma('sp', lambda e: e.dma_start(out=x_[:n, :], in_=xall[u * 128:u * 128 + n, cg * 512:(cg + 1) * 512]), key=xk, writes=[xk])
                        pi = nextpb(0, 4)
                        fns = []
                        for kc in range(16):
                            src = attT[:, kc, ocol(u):ocol(u) + n] if kc < 8 else yT[:, kc - 8, ocol(u):ocol(u) + n]
                            fns.append(mm(pb[pi][:n, :], src, wb4[:, kc, :], kc == 0, kc == 15))
                        grp('pe', fns, reads=['yT', 'wb4'], writes=['pb%d' % pi])
                        op('dve', lambda e: e.tensor_tensor(out=xmid[:n, u - OWN0, cg * 512:(cg + 1) * 512], in0=pb[pi][:n, :], in1=x_[:n, :], op=ALU.add),
                           reads=['pb%d' % pi, xk], writes=['xmid%d_%d' % (u, cg)])
                S.barrier()
            hfT = sb(p4, "hfT", [128, 16, NOWNCOL], BF16)
            with ExitStack() as p5:
                nfw_t = sb(p5, "nfw_t", [128, D], F32)
                hfb = [sb(p5, "hfb%d" % i, [128, D], BF16) for i in range(2)]
                s5 = sb(p5, "s5", [128, 40], F32)
                dma('sp', lambda e: e.dma_start(out=nfw_t[:], in_=nfw), key='nfw', writes=['nfw'])
                op('pool', lambda e: e.memset(s5[:], 0.0), writes=['s5'])
                for u in range(OWN0, NU):
                    n = nu(u)
                    i5 = u - OWN0
                    xm = xmid[:n, i5, :]
                    hb = hfb[u % 2]
                    op('act', lambda e: e.activation(out=junk[:n, :], in_=xm, func=AF.Square, accum_out=s5[:n, i5:i5 + 1]), reads=['s5'], writes=['junk', 's5a%d' % u])
                    rstd_from_ss(s5[:n, i5:i5 + 1], s5[:n, 20 + i5:21 + i5], s5[:n, 10 + i5:11 + i5], D, 's5a%d' % u, 's5r%d' % u)
                    if u == OWN0:
                        op('dve', lambda e: e.tensor_tensor(out=s5[:n, 20 + i5:21 + i5], in0=s5[:n, 20 + i5:21 + i5], in1=valid[:n, 0:1], op=ALU.mult),
                           reads=['s5r%d' % u, 'valid'], writes=['s5r%d' % u])
                    op('dve', lambda e: e.scalar_tensor_tensor(out=hb[:n, :], in0=xm, scalar=s5[:n, 20 + i5:21 + i5], in1=nfw_t[:n, :], op0=ALU.mult, op1=ALU.mult),
                       reads=['s5r%d' % u, 'nfw'], writes=['hfb%d' % (u % 2)])
                    for half in range(2):
                        pi = nextpb(0, 4)
                        grp('pe', [tp(pbh[pi][:, j * 128:j * 128 + n], hb[:n, (half * 8 + j) * 128:(half * 8 + j + 1) * 128], ident[:n, :n]) for j in range(8)],
                            reads=['hfb%d' % (u % 2), 'ident'], writes=['pb%d' % pi])
                        evac(hfT[:, half * 8:half * 8 + 8, ocol(u):ocol(u) + n], pbh[pi][:, :].rearrange("p (j t) -> p j t", j=8)[:, :, :n],
                             reads=['pb%d' % pi], writes=['hfT%d_%d' % (u, half)])
                S.barrier()
            with ExitStack() as p6:
                wg = [sb(p6, "wg%d" % i, [128, 16, 128], BF16) for i in range(2)]
                wu = [sb(p6, "wu%d" % i, [128, 16, 128], BF16) for i in range(2)]
                wd = sb(p6, "wd", [128, 4, D], BF16)
                aT = sb(p6, "aT", [128, 4, NOWNCOL], BF16)
                gbuf = sb(p6, "gbuf", [128, 2 + 1152], F32)
                gsb = sb(p6, "gsb", [128, 16, 6], F32)
                acc6 = sb(p6, "acc6", [128, NOWNCOL], F32)
                sg6 = sb(p6, "sg6", [128, NOWNCOL], F32)
                cfw = sb(p6, "cfw", [128, NFC, 3], F32)
                cfb = sb(p6, "cfb", [128, NFC], F32)
                cfo = sb(p6, "cfo", [128, NFC, 2], F32)
                stf = [sb(p6, "stf%d" % i, [32, 128], F32) for i in range(2)]
                gtk = [sb(p6, "gtk%d" % i, [64, 128], F32) for i in range(2)]
                dma('sp', lambda e: e.dma_start(out=cfw[:], in_=cfwd), key='c0', writes=['cfw'])
                dma('sp', lambda e: e.dma_start(out=cfb[:], in_=cfbd), key='c1', writes=['cfb'])
                op('pool', lambda e: e.memset(gbuf[:, 0:2], 0.0), writes=['gbuf0'])
                tg = [(0, 512), (512, 512), (1024, 192)]
                for fc in range(NFC if KSTAGE >= 6 else 0):
                    k = fc % 2
                    fs = slice(fc * 128, (fc + 1) * 128)
                    dma('pool', lambda e: e.dma_start(out=wg[k][:, :, :], in_=w_gate[:, fs].rearrange("(kc p) n -> p kc n", p=128)), key='wg%d' % k, writes=['wg%d' % k])
                    dma('pool', lambda e: e.dma_start(out=wu[k][:, :, :], in_=w_up[:, fs].rearrange("(kc p) n -> p kc n", p=128)), key='wu%d' % k, writes=['wu%d' % k])
                    dma('sp', lambda e: e.dma_start(out=stf[k][:, :], in_=st_cffn[:, fs]), key='stf%d' % k, writes=['stf%d' % k])
                    pi = 6
                    grp('pe', [tp(pb[pi][:, 0:32], stf[k][0:32, :], identf[0:32, 0:32])], reads=['stf%d' % k, 'identf'], writes=['pb%d' % pi])
                    evac(gsb[:, :, 0:2], pb[pi][:, 0:32].rearrange("p (b t) -> p b t", t=2), reads=['pb%d' % pi], writes=['gsb_s'], eng='act')
                    pi = 7
                    grp('pe', [mm(pb[pi][:64, 0:128], hfT[:, kc, 1152:1216], wg[k][:, kc, :], kc == 0, kc == 15) for kc in range(16)], reads=['wg%d' % k], writes=['pb%d' % pi])
                    evac(gtk[k][:, :], pb[pi][:64, 0:128], reads=['pb%d' % pi], writes=['gtk%d' % k], eng='act')
                    for t in (2, 3):
                        dma('sp', lambda e: e.dma_start(out=CFFNS[:, t - 2, fs], in_=gtk[k][t:64:4, :]), key='gtk%d' % k, reads=['gtk%d' % k])
                    for gi, (g0, ng) in enumerate(tg):
                        grp('pe', [mm(pb[gi][:, :ng], wg[k][:, kc, :], hfT[:, kc, g0:g0 + ng], kc == 0, kc == 15) for kc in range(16)], reads=['wg%d' % k], writes=['pb%d' % gi])
                        if gi < 2:
                            evac(gbuf[:, 2 + g0:2 + g0 + ng], pb[gi][:, :ng], reads=['pb%d' % gi], writes=['gbuf%d' % (gi + 1)], eng='act')
                        else:
                            evac(gbuf[:, 2 + 1024:2 + 1152], pb[gi][:, 0:128], reads=['pb%d' % gi], writes=['gbuf3'], eng='act')
                            evac(gsb[:, :, 2:6], pb[gi][:, 128:192].rearrange("p (b t) -> p b t", t=4), reads=['pb%d' % gi], writes=['gsb_n'], eng='act')
                        grp('pe', [mm(pb[3 + gi][:, :ng], wu[k][:, kc, :], hfT[:, kc, g0:g0 + ng], kc == 0, kc == 15) for kc in range(16)], reads=['wu%d' % k], writes=['pb%d' % (3 + gi)])
                    gk = ['gbuf0', 'gbuf1', 'gbuf2', 'gbuf3']
                    op('act', lambda e: e.copy(out=cfo[:, fc, :], in_=gbuf[:, 1152:1154]), reads=gk, writes=['cfo'])
                    op('dve', lambda e: e.tensor_scalar(out=acc6[:, 0:1152], in0=gbuf[:, 2:1154], scalar1=cfw[:, fc, 2:3], scalar2=cfb[:, fc:fc + 1], op0=ALU.mult, op1=ALU.add),
                       reads=gk + ['cfw', 'cfb'], writes=['acc6'])
                    for i in (1, 0):
                        op('dve', lambda e: e.scalar_tensor_tensor(out=acc6[:, 0:1152], in0=gbuf[:, i:i + 1152], scalar=cfw[:, fc, i:i + 1], in1=acc6[:, 0:1152], op0=ALU.mult, op1=ALU.add),
                           reads=gk + ['cfw', 'acc6'], writes=['acc6'])
                    a4 = acc6[:, 1152:1216].rearrange("p (b t) -> p b t", t=4)
                    op('dve', lambda e: e.tensor_scalar(out=a4, in0=gsb[:, :, 2:6], scalar1=cfw[:, fc, 2:3], scalar2=cfb[:, fc:fc + 1], op0=ALU.mult, op1=ALU.add),
                       reads=['gsb_s', 'gsb_n', 'cfw', 'cfb', 'acc6'], writes=['acc6'])
                    for i in (1, 0):
                        op('dve', lambda e: e.scalar_tensor_tensor(out=a4, in0=gsb[:, :, i:i + 4], scalar=cfw[:, fc, i:i + 1], in1=a4, op0=ALU.mult, op1=ALU.add),
                           reads=['gsb_s', 'gsb_n', 'cfw', 'acc6'], writes=['acc6'])
                    op('act', lambda e: e.activation(out=sg6[:, :], in_=acc6[:, :], func=AF.Silu), reads=['acc6'], writes=['sg6'])
                    for gi, (g0, ng) in enumerate(tg):
                        op('dve', lambda e: e.tensor_tensor(out=aT[:, fc % 4, g0:g0 + ng], in0=pb[3 + gi][:, :ng], in1=sg6[:, g0:g0 + ng], op=ALU.mult),
                           reads=['pb%d' % (3 + gi), 'sg6'], writes=['aT%d_%d' % (fc % 4, gi)])
                    if fc % 4 == 3:
                        ps_ = fc // 4
                        dma('pool', lambda e: e.dma_start(out=wd[:, :, :], in_=w_down[ps_ * 512:(ps_ + 1) * 512, :].rearrange("(j p) n -> p j n", p=128)), key='wd', writes=['wd'])
                        for u in range(OWN0, NU):
                            n = nu(u)
                            for cb in range(4):
                                pi = 6 + (cb % 2)
                                grp('pe', [mm(pb[pi][:n, :], aT[:, j, ocol(u):ocol(u) + n], wd[:, j, cb * 512:(cb + 1) * 512], j == 0, j == 3) for j in range(4)],
                                    reads=['wd'] + ['aT%d_%d' % (j, gi) for j in range(4) for gi in range(3)], writes=['pb%d' % pi])
                                xv = xmid[:n, u - OWN0, cb * 512:(cb + 1) * 512]
                                op('dve', lambda e: e.tensor_tensor(out=xv, in0=pb[pi][:n, :], in1=xv, op=ALU.add), reads=['pb%d' % pi], writes=['xm%d_%d' % (u, cb)])
                for t in range(2):
                    dma('sp', lambda e: e.dma_start(out=CFFNP[t].rearrange("(j f) -> f j", f=128), in_=cfo[:, :, t], allow_slow_non_contiguous=True), key='cfo', reads=['cfo'])
                S.barrier()
            with ExitStack() as p7:
                nlw_t = sb(p7, "nlw_t", [128, D], F32)
                yo = [sb(p7, "yo%d" % i, [128, D], F32) for i in range(2)]
                s7 = sb(p7, "s7", [128, 40], F32)
                dma('sp', lambda e: e.dma_start(out=nlw_t[:], in_=nlw), key='nlw', writes=['nlw'])
                op('pool', lambda e: e.memset(s7[:], 0.0), writes=['s7'])
                for u in range(8, NU):
                    n = nu(u)
                    i5 = u - OWN0
                    xm = xmid[:n, i5, :]
                    y_ = yo[u % 2]
                    op('act', lambda e: e.activation(out=junk[:n, :], in_=xm, func=AF.Square, accum_out=s7[:n, i5:i5 + 1]), reads=['s7'], writes=['junk', 's7a%d' % u])
                    rstd_from_ss(s7[:n, i5:i5 + 1], s7[:n, 20 + i5:21 + i5], s7[:n, 10 + i5:11 + i5], D, 's7a%d' % u, 's7r%d' % u)
                    op('dve', lambda e: e.scalar_tensor_tensor(out=y_[:n, :], in0=xm, scalar=s7[:n, 20 + i5:21 + i5], in1=nlw_t[:n, :], op0=ALU.mult, op1=ALU.mult),
                       reads=['s7r%d' % u, 'nlw'], writes=['yo%d' % (u % 2)])
                    dst = YP[(u - 8) * 128:(u - 8) * 128 + n, :] if u < 16 else YS
                    dma('sp', lambda e: e.dma_start(out=dst, in_=y_[:n, :]), key='yo%d' % (u % 2), reads=['yo%d' % (u % 2)])
        S.barrier(['sp'])
    return nc


_PROG = {}


def _consts():
    ident = np.eye(128, dtype=np.float32)
    r = np.arange(128)
    triu = (r[:, None] <= r[None, :]).astype(np.float32)
    nmask = np.where(r[:, None] > r[None, :], NEG, 0.0).astype(np.float32)
    r4 = np.arange(4)
    mk4 = np.where(r4[:, None] > r4[None, :], NEG, 0.0).astype(np.float32)
    return ident, triu, nmask, mk4


def _rope_tables(half):
    T0 = half * 1024
    pos = np.concatenate([T0 - 1024 + np.arange(2048), 2048 + (np.arange(64) % 4)]).astype(np.float32)
    inv = (np.float32(500000.0) ** (-(np.arange(8, dtype=np.float32) / np.float32(8)))).astype(np.float32)
    ang = (pos[:, None] * inv[None, :]).astype(np.float32)
    cos = np.zeros((NU * 128, 8), np.float32)
    sin = np.zeros((NU * 128, 8), np.float32)
    cos[:NCOL] = np.cos(ang)
    sin[:NCOL] = np.sin(ang)
    cos = np.ascontiguousarray(cos.reshape(NU, 128, 8).transpose(1, 0, 2))
    sin = np.ascontiguousarray(sin.reshape(NU, 128, 8).transpose(1, 0, 2))
    return cos, sin


def kernel(**inp):
    f32 = np.float32
    g = lambda k: np.asarray(inp[k])
    x_prompt, x_sample = g("x_prompt"), g("x_sample")
    ck = np.ascontiguousarray(g("cache_k")[0].reshape(NPHYS_ROWS, 1024))
    cv = np.ascontiguousarray(g("cache_v")[0].reshape(NPHYS_ROWS, 1024))
    ident, triu, nmask, mk4 = _consts()
    rep = lambda a: np.ascontiguousarray(np.broadcast_to(np.asarray(a, f32).reshape(1, -1), (128, np.asarray(a).size)))
    lam = np.stack([g("lambda_q1")[0], g("lambda_k1")[0], g("lambda_q2")[0], g("lambda_k2")[0]])
    common = {
        "iota": np.arange(128, dtype=f32).reshape(128, 1), "identd": ident, "triud": triu, "nmaskd": nmask, "maskn4d": mk4,
        "cache_k": ck, "cache_v": cv,
        "w_in": g("w_in")[0], "w_out": g("w_out")[0], "w_gate": g("w_gate")[0], "w_up": g("w_up")[0], "w_down": g("w_down")[0],
        "nmw": rep(g("norm_mix_w")[0]), "nfw": rep(g("norm_ffn_w")[0]), "nlw": rep(g("norm_final_w")),
        "nsw": rep(g("norm_ssd_w")[0]), "sublw": rep(g("subln_w")[0]),
        "lamd": np.ascontiguousarray(np.broadcast_to(lam[None], (128, 4, 64))).astype(f32),
        "dtbd": rep(g("dt_bias")[0]), "alogd": rep(g("a_log")[0]), "dskd": rep(g("d_skip")[0]),
        "cswd": np.ascontiguousarray(g("conv_ssd_w")[0].T.reshape(12, 128, 4).transpose(1, 0, 2)),
        "csbd": np.ascontiguousarray(g("conv_ssd_b")[0].reshape(12, 128).T),
        "cfwd": np.ascontiguousarray(g("conv_ffn_w")[0].T.reshape(NFC, 128, 3).transpose(1, 0, 2)),
        "cfbd": np.ascontiguousarray(g("conv_ffn_b")[0].reshape(NFC, 128).T),
    }
    in_maps = []
    for c in range(8):
        b, half = c // 2, c % 2
        xall = np.zeros((NCOL, D), f32)
        if half == 0:
            xall[1024:2048] = x_prompt[b, 0:1024]
        else:
            xall[0:2048] = x_prompt[b]
        xall[2048:] = x_sample[c * 16:(c + 1) * 16].reshape(64, D)
        cos, sin = _rope_tables(half)
        m = dict(common)
        m.update({
            "xall": xall, "ropec": cos, "ropes": sin,
            "valid": np.full((128, 1), float(half), f32),
            "ptb": np.ascontiguousarray(np.broadcast_to(g("page_table")[c * 16:(c + 1) * 16].reshape(1, 256), (128, 256))).astype(np.int32),
            "st_ssm": np.ascontiguousarray(g("state_ssm")[0, c * 16:(c + 1) * 16]),
            "st_cssd": np.ascontiguousarray(g("state_conv_ssd")[0, c * 16:(c + 1) * 16].reshape(48, 1536)),
            "st_cffn": np.ascontiguousarray(g("state_conv_ffn")[0, c * 16:(c + 1) * 16].reshape(32, DFF)),
        })
        in_maps.append(m)
    if "nc" not in _PROG:
        _PROG["nc"] = build_program()
    res = run_bass_kernel_spmd(_PROG["nc"], in_maps, core_ids=list(range(8))).results
    y_p = np.zeros((4, 2048, D), f32); k_p = np.zeros((1, 4, 2048, 8, 2, 64), f32); v_p = np.zeros((1, 4, 2048, 8, 128), f32)
    ssm_p = np.zeros((1, 4, 16, 64, 128), f32); cs_p = np.zeros((1, 4, 3, 1536), f32); cf_p = np.zeros((1, 4, 2, DFF), f32)
    y_s = np.zeros((128, 4, D), f32); k_s = np.zeros((1, 128, 4, 8, 2, 64), f32); v_s = np.zeros((1, 128, 4, 8, 128), f32)
    ssm_s = np.zeros((1, 128, 16, 64, 128), f32); cs_s = np.zeros((1, 128, 3, 1536), f32); cf_s = np.zeros((1, 128, 2, DFF), f32)
    for c in range(8):
        b, half = c // 2, c % 2
        r = res[c]
        sl = slice(half * 1024, half * 1024 + 1024)
        y_p[b, sl] = r["YP"]
        k_p[0, b, sl] = r["KP"].reshape(1024, 8, 2, 64)
        v_p[0, b, sl] = r["VP"].reshape(1024, 8, 128)
        if half == 1:
            ssm_p[0, b] = r["SSMP"]; cs_p[0, b] = r["CSSDP"]; cf_p[0, b] = r["CFFNP"]
        bs = slice(c * 16, c * 16 + 16)
        y_s[bs] = r["YS"].reshape(16, 4, D)
        k_s[0, bs] = r["KS"].reshape(16, 4, 8, 2, 64)
        v_s[0, bs] = r["VS"].reshape(16, 4, 8, 128)
        ssm_s[0, bs] = r["SSMS"]; cs_s[0, bs] = r["CSSDS"]; cf_s[0, bs] = r["CFFNS"]
    return (y_p, y_s, k_p, v_p, ssm_p, cs_p, cf_p, k_s, v_s, ssm_s, cs_s, cf_s)
```
